# Optimizing a Trainium2 kernel written in Bass

```python
import math
import jax, jax.numpy as jnp
from jax import lax
import numpy as np

D_MODEL = 1024
BATCH = 8
SEQ = 8192
DEPTH = 1

LRU_WIDTH = D_MODEL
LRU_BLOCKS = 16
LRU_BLOCK_W = LRU_WIDTH // LRU_BLOCKS
LRU_C = 8.0
CONV_K = 4
SSD_HEAD_DIM = 64
SSD_INNER = D_MODEL
SSD_HEADS = SSD_INNER // SSD_HEAD_DIM
SSD_GROUPS = 2
SSD_HPG = SSD_HEADS // SSD_GROUPS
SSD_STATE = 128
SSD_CHUNK = 128
MIX_WIDTH = LRU_WIDTH + SSD_INNER
SPLITS = (LRU_WIDTH, LRU_WIDTH, SSD_INNER, SSD_INNER,
          SSD_GROUPS * SSD_STATE, SSD_GROUPS * SSD_STATE, SSD_HEADS)
IN_COLS = sum(SPLITS)
SSD_CONV_CH = SSD_INNER + 2 * SSD_GROUPS * SSD_STATE
D_FF = -(-8 * D_MODEL // (3 * 256)) * 256
EPS = 1e-6

kernel_name = "hymba_style_rglru_ssd_hybrid"


def rms_norm(x, w):
    xf = x.astype(jnp.float32)
    y = xf * lax.rsqrt(jnp.mean(xf * xf, axis=-1, keepdims=True) + EPS)
    return (y * w.astype(jnp.float32)).astype(x.dtype)


def causal_depthwise_conv(x, w, b):
    K = w.shape[0]
    T = x.shape[1]
    xp = jnp.pad(x, ((0, 0), (K - 1, 0), (0, 0)))
    y = b + xp[:, 0:T] * w[0]
    for k in range(1, K):
        y = y + xp[:, k:k + T] * w[k]
    return y


def rg_lru(x, w_a, b_a, w_x, b_x, lam):
    bsz, T, W = x.shape
    xf = x.astype(jnp.float32)
    xb = xf.reshape(bsz, T, LRU_BLOCKS, LRU_BLOCK_W)
    r = jax.nn.sigmoid(jnp.einsum("btki,kij->btkj", xb, w_a.astype(jnp.float32)).reshape(bsz, T, W) + b_a.astype(jnp.float32))
    i = jax.nn.sigmoid(jnp.einsum("btki,kij->btkj", xb, w_x.astype(jnp.float32)).reshape(bsz, T, W) + b_x.astype(jnp.float32))
    log_a = -LRU_C * r * jax.nn.softplus(-lam.astype(jnp.float32))
    a = jnp.exp(log_a)
    u = jnp.sqrt(-jnp.expm1(2.0 * log_a)) * (i * xf)

    def combine(left, right):
        a1, b1 = left
        a2, b2 = right
        return a1 * a2, a2 * b1 + b2

    _, h = lax.associative_scan(combine, (a, u), axis=1)
    return h.astype(x.dtype)


def segsum(a):
    L = a.shape[-1]
    cs = jnp.cumsum(a, axis=-1)
    diff = cs[..., :, None] - cs[..., None, :]
    mask = jnp.tril(jnp.ones((L, L), dtype=bool))
    return jnp.where(mask, diff, -jnp.inf)


def ssd_chunked(xs, a, Bm, Cm):
    b, t, g, e, p = xs.shape
    n = Bm.shape[-1]
    c = t // SSD_CHUNK
    xs = xs.reshape(b, c, SSD_CHUNK, g, e, p)
    Bm = Bm.reshape(b, c, SSD_CHUNK, g, n)
    Cm = Cm.reshape(b, c, SSD_CHUNK, g, n)
    a = a.reshape(b, c, SSD_CHUNK, g, e).transpose(0, 3, 4, 1, 2)
    a_cs = jnp.cumsum(a, axis=-1)
    Lmat = jnp.exp(segsum(a))
    scores = jnp.einsum("bclgn,bcsgn->bgcls", Cm, Bm)
    M = scores[:, :, None] * Lmat
    y_diag = jnp.einsum("bgecls,bcsgep->bclgep", M, xs)
    decay_states = jnp.exp(a_cs[..., -1:] - a_cs)
    states = jnp.einsum("bclgn,bgecl,bclgep->bcgepn", Bm, decay_states, xs)
    chunk_a = jnp.pad(a_cs[..., -1], ((0, 0), (0, 0), (0, 0), (1, 0)))
    decay_chunk = jnp.exp(segsum(chunk_a))
    states = jnp.concatenate([jnp.zeros_like(states[:, :1]), states], axis=1)
    prev_states = jnp.einsum("bgezc,bcgepn->bzgepn", decay_chunk, states)[:, :-1]
    y_off = jnp.einsum("bclgn,bcgepn,bgecl->bclgep", Cm, prev_states, jnp.exp(a_cs))
    return (y_diag + y_off).reshape(b, t, g, e, p)


def setup_inputs(seed: int = 0) -> dict:
    key = jax.random.key(seed)
    ks = jax.random.split(key, 24)
    f32 = jnp.float32
    L = DEPTH

    def nrm(k, shape, scale):
        return jax.random.normal(k, shape, f32) * scale

    def gain(k, shape):
        return 1.0 + 0.05 * jax.random.normal(k, shape, f32)

    x = jax.random.normal(ks[0], (BATCH, SEQ, D_MODEL), f32)
    a_init = jax.random.uniform(ks[9], (L, LRU_WIDTH), f32, 0.9, 0.999)
    s = a_init ** (1.0 / LRU_C)
    lru_lambda = jnp.log(s) - jnp.log1p(-s)
    dt0 = jnp.exp(jax.random.uniform(ks[13], (L, SSD_HEADS), f32, math.log(1e-3), math.log(1e-1)))
    ssd_dt_bias = dt0 + jnp.log(-jnp.expm1(-dt0))
    ssd_a_log = jnp.log(jax.random.uniform(ks[14], (L, SSD_HEADS), f32, 1.0, 16.0))
    return {
        "x": x,
        "pre_mix_norm": gain(ks[1], (L, D_MODEL)),
        "w_in": nrm(ks[2], (L, D_MODEL, IN_COLS), D_MODEL ** -0.5),
        "lru_conv_w": nrm(ks[3], (L, CONV_K, LRU_WIDTH), CONV_K ** -0.5),
        "lru_conv_b": nrm(ks[4], (L, LRU_WIDTH), 0.01),
        "lru_wa": nrm(ks[5], (L, LRU_BLOCKS, LRU_BLOCK_W, LRU_BLOCK_W), LRU_BLOCK_W ** -0.5),
        "lru_ba": nrm(ks[6], (L, LRU_WIDTH), 0.01),
        "lru_wx": nrm(ks[7], (L, LRU_BLOCKS, LRU_BLOCK_W, LRU_BLOCK_W), LRU_BLOCK_W ** -0.5),
        "lru_bx": nrm(ks[8], (L, LRU_WIDTH), 0.01),
        "lru_lambda": lru_lambda,
        "lru_out_norm": gain(ks[10], (L, LRU_WIDTH)),
        "ssd_conv_w": nrm(ks[11], (L, CONV_K, SSD_CONV_CH), CONV_K ** -0.5),
        "ssd_conv_b": nrm(ks[12], (L, SSD_CONV_CH), 0.01),
        "ssd_dt_bias": ssd_dt_bias,
        "ssd_a_log": ssd_a_log,
        "ssd_d": gain(ks[15], (L, SSD_HEADS)),
        "ssd_out_norm": gain(ks[16], (L, SSD_INNER)),
        "w_out": nrm(ks[17], (L, MIX_WIDTH, D_MODEL), MIX_WIDTH ** -0.5),
        "post_mix_norm": gain(ks[18], (L, D_MODEL)),
        "pre_ffn_norm": gain(ks[19], (L, D_MODEL)),
        "w_gate": nrm(ks[20], (L, D_MODEL, D_FF), D_MODEL ** -0.5),
        "w_up": nrm(ks[21], (L, D_MODEL, D_FF), D_MODEL ** -0.5),
        "w_down": nrm(ks[22], (L, D_FF, D_MODEL), D_FF ** -0.5),
        "post_ffn_norm": gain(ks[23], (L, D_MODEL)),
    }


def reference(x, pre_mix_norm, w_in, lru_conv_w, lru_conv_b, lru_wa, lru_ba, lru_wx, lru_bx,
              lru_lambda, lru_out_norm, ssd_conv_w, ssd_conv_b, ssd_dt_bias, ssd_a_log, ssd_d,
              ssd_out_norm, w_out, post_mix_norm, pre_ffn_norm, w_gate, w_up, w_down, post_ffn_norm):
    bsz, T, _ = x.shape
    offs = np.cumsum((0,) + SPLITS)
    for li in range(DEPTH):
        h = rms_norm(x, pre_mix_norm[li])
        proj = h @ w_in[li]
        lru_x = proj[..., offs[0]:offs[1]]
        lru_gate = proj[..., offs[1]:offs[2]]
        ssd_z = proj[..., offs[2]:offs[3]]
        ssd_xbc = proj[..., offs[3]:offs[6]]
        ssd_dt = proj[..., offs[6]:offs[7]]

        lx = causal_depthwise_conv(lru_x, lru_conv_w[li], lru_conv_b[li])
        lh = rg_lru(lx, lru_wa[li], lru_ba[li], lru_wx[li], lru_bx[li], lru_lambda[li])
        y_lru = rms_norm(lh * jax.nn.gelu(lru_gate), lru_out_norm[li])

        xbc = jax.nn.silu(causal_depthwise_conv(ssd_xbc, ssd_conv_w[li], ssd_conv_b[li]))
        sx = xbc[..., :SSD_INNER].astype(jnp.float32).reshape(bsz, T, SSD_GROUPS, SSD_HPG, SSD_HEAD_DIM)
        sB = xbc[..., SSD_INNER:SSD_INNER + SSD_GROUPS * SSD_STATE].astype(jnp.float32).reshape(bsz, T, SSD_GROUPS, SSD_STATE)
        sC = xbc[..., SSD_INNER + SSD_GROUPS * SSD_STATE:].astype(jnp.float32).reshape(bsz, T, SSD_GROUPS, SSD_STATE)
        dt = jax.nn.softplus(ssd_dt.astype(jnp.float32) + ssd_dt_bias[li].astype(jnp.float32))
        dt = dt.reshape(bsz, T, SSD_GROUPS, SSD_HPG)
        A = -jnp.exp(ssd_a_log[li].astype(jnp.float32)).reshape(SSD_GROUPS, SSD_HPG)
        y = ssd_chunked(sx * dt[..., None], dt * A, sB, sC)
        y = y + ssd_d[li].astype(jnp.float32).reshape(SSD_GROUPS, SSD_HPG)[..., None] * sx
        y = y.reshape(bsz, T, SSD_INNER).astype(x.dtype)
        y_ssd = rms_norm(y * jax.nn.silu(ssd_z), ssd_out_norm[li])

        mix = jnp.concatenate([y_lru, y_ssd], axis=-1) @ w_out[li]
        x = x + rms_norm(mix, post_mix_norm[li])

        h = rms_norm(x, pre_ffn_norm[li])
        f = (jax.nn.silu(h @ w_gate[li]) * (h @ w_up[li])) @ w_down[li]
        x = x + rms_norm(f, post_ffn_norm[li])
    return x
```

```python
from contextlib import ExitStack
import numpy as np
import concourse.bass as bass
import concourse.mybir as mybir
from concourse.bass_utils import run_bass_kernel_spmd

F32 = mybir.dt.float32
BF16 = mybir.dt.bfloat16
AF = mybir.ActivationFunctionType
ALU = mybir.AluOpType

D = 1024
DFF = 2816
NF = DFF // 128
EPS = 1e-6
EPOCH = 30000


class Buf:
    __slots__ = ("name", "w", "r", "ps_idx", "ps_live")

    def __init__(self, name):
        self.name = name
        self.w = None
        self.r = {}
        self.ps_idx = None
        self.ps_live = False


class K:
    def __init__(self, nc, es):
        self.nc = nc
        self.es = es
        self.eng = {"pe": nc.tensor, "dve": nc.vector, "act": nc.scalar, "pool": nc.gpsimd, "sp": nc.sync}
        self.cnt = {n: 0 for n in self.eng}
        self.sems = {}
        self.seen = {n: {} for n in self.eng}
        self.dcnt = {}
        self.stores = []
        self.psfree = []

    def _sem(self, key):
        if key not in self.sems:
            self.sems[key] = self.es.enter_context(self.nc.semaphore("s_%s_%d" % key))
        return self.sems[key]

    def _wait(self, en, evs):
        best = {}
        for ev in evs:
            if ev is None:
                continue
            key, val = ev
            if self.seen[en].get(key, 0) >= val:
                continue
            if best.get(key, 0) < val:
                best[key] = val
        for key, val in best.items():
            self.eng[en].wait_ge(self._sem(key), val)
            self.seen[en][key] = val

    def _deps(self, en, R, W):
        evs = []
        for b in R:
            evs.append(b.w)
        for b in W:
            if b.w is not None and not (b.w[0][0] == en and en == "pe"):
                evs.append(b.w)
            for e2, ev in b.r.items():
                if e2 != en:
                    evs.append(ev)
        return evs

    def op(self, en, fn, R=(), W=()):
        self._wait(en, self._deps(en, R, W))
        ins = fn(self.eng[en])
        self.cnt[en] += 1
        c = self.cnt[en]
        ep = (c - 1) // EPOCH
        key = (en, ep)
        val = c - ep * EPOCH
        ins.then_inc(self._sem(key), 1)
        ev = (key, val)
        for b in R:
            b.r[en] = ev
        for b in W:
            b.w = ev
            b.r = {}
        return ev

    def dma(self, q, out, in_, R=(), W=(), key=None, store=False):
        evs = []
        for b in R:
            evs.append(b.w)
        for b in W:
            if b.w is not None and b.w[0][0] != "d_" + key:
                evs.append(b.w)
            for e2, ev in b.r.items():
                evs.append(ev)
        self._wait(q, evs)
        ins = self.eng[q].dma_start(out=out, in_=in_)
        c = self.dcnt.get(key, 0) + 16
        ep = c // EPOCH
        if ep != self.dcnt.get(key, 0) // EPOCH:
            c = ep * EPOCH + 16
        self.dcnt[key] = c
        skey = ("d_" + key, ep)
        val = c - ep * EPOCH
        ins.then_inc(self._sem(skey), 16)
        ev = (skey, val)
        for b in R:
            b.r["d_" + key] = ev
        for b in W:
            b.w = ev
            b.r = {}
        if store:
            self.stores.append(ev)
        return ev

    def barrier(self):
        evs = []
        for n, c in self.cnt.items():
            if c > 0:
                ep = (c - 1) // EPOCH
                evs.append(((n, ep), c - ep * EPOCH))
        for key, c in self.dcnt.items():
            ep = c // EPOCH
            evs.append((("d_" + key, ep), c - ep * EPOCH))
        for n in ("pe", "dve", "act", "sp", "pool"):
            self._wait(n, evs)


def build(T, dbg=False, sweeps=(1, 2, 3)):
    nc = bass.Bass("TRN2", target_bir_lowering=False)
    NCH = T // 128
    x_d = nc.dram_tensor("x", [T, D], F32, kind="ExternalInput").ap()
    win_l_d = nc.dram_tensor("w_in_l", [128, 8, 2048], F32, kind="ExternalInput").ap()
    win_s_d = nc.dram_tensor("w_in_s", [128, 8, 2576], F32, kind="ExternalInput").ap()
    wout_l_d = nc.dram_tensor("w_out_l", [128, 8, 1024], F32, kind="ExternalInput").ap()
    wout_s_d = nc.dram_tensor("w_out_s", [128, 8, 1024], F32, kind="ExternalInput").ap()
    wgu_d = nc.dram_tensor("w_gu", [NF, 128, 8, 256], F32, kind="ExternalInput").ap()
    wdn_d = nc.dram_tensor("w_down", [128, NF, 1024], F32, kind="ExternalInput").ap()
    wa_d = nc.dram_tensor("wa", [128, 8, 128], F32, kind="ExternalInput").ap()
    wx_d = nc.dram_tensor("wx", [128, 8, 128], F32, kind="ExternalInput").ap()
    NPP = 156
    pp_d = nc.dram_tensor("pp", [128, NPP], F32, kind="ExternalInput").ap()
    NBC = 2096
    bc_d = nc.dram_tensor("bc", [128, NBC], F32, kind="ExternalInput").ap()
    cst_d = nc.dram_tensor("cst", [128, 4, 128], F32, kind="ExternalInput").ap()
    out_d = nc.dram_tensor("out", [T, D], F32, kind="ExternalOutput").ap()
    pl_d = nc.dram_tensor("pl_scr", [T, D], F32, kind="ExternalOutput" if dbg else "Internal").ap()
    x1_d = nc.dram_tensor("x1_scr", [T, D], F32, kind="ExternalOutput" if dbg else "Internal").ap()
    wgu_s = nc.dram_tensor("wgu_scr", [NF, 128, 8 * 256], BF16, kind="Internal").ap()

    with ExitStack() as es:
        k = K(nc, es)

        def sb(name, shape, dt):
            return es.enter_context(nc.sbuf_tensor("sb_" + name, shape, dt))

        pp = sb("pp", [128, NPP], F32)
        bc = sb("bc", [128, NBC], F32)
        cst = sb("cst", [128, 4, 128], F32)
        cstb = sb("cstb", [128, 4, 128], BF16)
        der = sb("der", [128, 64], F32)
        b_pp, b_bc, b_cst, b_cstb, b_der = Buf("pp"), Buf("bc"), Buf("cst"), Buf("cstb"), Buf("der")
        k.dma("sp", pp[:], pp_d, W=[b_pp], key="pp")
        k.dma("sp", bc[:], bc_d, W=[b_bc], key="bc")
        k.dma("sp", cst[:], cst_d, W=[b_cst], key="cst")
        k.op("dve", lambda e: e.tensor_copy(out=cstb[:], in_=cst[:]), R=[b_cst], W=[b_cstb])
        ident_f, U_f, SL_f, ones_f = cst[:, 0, :], cst[:, 1, :], cst[:, 2, :], cst[:, 3, :]
        ident_b, ones_b = cstb[:, 0, :], cstb[:, 3, :]
        G1, G2, GL, GS, LCW, LCB, BA, BX, LAM, SCW, SCB = 0, 8, 16, 24, 32, 64, 72, 80, 88, 96, 144
        GP1, GP2, DTB, ALOG, DSK = 0, 1024, 2048, 2064, 2080
        k.op("act", lambda e: e.activation(out=der[:, 32:40], in_=pp[:, LAM:LAM + 8], func=AF.Exp, scale=-1.0), R=[b_pp], W=[b_der])
        k.op("act", lambda e: e.activation(out=der[:, 40:48], in_=der[:, 32:40], func=AF.Ln, bias=1.0), R=[b_der], W=[b_der])
        k.op("dve", lambda e: e.tensor_scalar(out=der[:, 0:8], in0=der[:, 40:48], scalar1=-8.0, scalar2=None, op0=ALU.mult), R=[b_der], W=[b_der])
        k.op("dve", lambda e: e.tensor_scalar(out=der[:, 8:16], in0=der[:, 40:48], scalar1=-16.0, scalar2=None, op0=ALU.mult), R=[b_der], W=[b_der])
        k.op("act", lambda e: e.activation(out=der[:, 48:64], in_=bc[:, ALOG:ALOG + 16], func=AF.Exp), R=[b_bc, b_der], W=[b_der])
        k.op("dve", lambda e: e.tensor_scalar(out=der[:, 16:32], in0=der[:, 48:64], scalar1=-1.0, scalar2=None, op0=ALU.mult), R=[b_der], W=[b_der])

        psb = [es.enter_context(nc.psum_tensor("ps%d" % i, [128, 512], F32)) for i in range(8)]
        psbuf = [Buf("ps%d" % i) for i in range(8)]
        psn = [0]

        k.psfree = list(range(8))

        def ps_next():
            i = k.psfree.pop(0)
            k.psfree.append(i)
            return psb[i], psbuf[i]

        def ps_hold():
            i = k.psfree.pop(0)
            return psb[i], psbuf[i], i

        def ps_release(i):
            k.psfree.append(i)

        def mm(out, lhsT, rhs, start, stop, R, W):
            k.op("pe", lambda e: e.matmul(out, lhsT, rhs, start=start, stop=stop), R=R, W=W)

        xq = [sb("xq%d" % i, [128, D], F32) for i in range(2)]
        b_xq = [Buf("xq%d" % i) for i in range(2)]
        hq = [sb("hq%d" % i, [128, D], BF16) for i in range(2)]
        b_hq = [Buf("hq%d" % i) for i in range(2)]
        hT = [sb("hT%d" % i, [128, 8, 512], BF16) for i in range(2)]
        b_hT = [Buf("hT%d" % i) for i in range(2)]
        st = sb("st", [128, 16], F32)
        b_st = [Buf("st%d" % i) for i in range(16)]
        stn = [0]
        fq = [0]

        def st_next():
            i = stn[0] % 16
            stn[0] += 1
            return st[:, i:i + 1], b_st[i]

        def rstd_of(ss_ap, b_ss, n):
            r_ap, b_r = st_next()
            k.op("act", lambda e: e.activation(out=r_ap, in_=ss_ap, func=AF.Sqrt, scale=1.0 / n, bias=b_eps_ap), R=[b_ss, b_der2], W=[b_r])
            r2_ap, b_r2 = st_next()
            k.op("dve", lambda e: e.reciprocal(out=r2_ap, in_=r_ap), R=[b_r], W=[b_r2])
            return r2_ap, b_r2

        der2 = sb("der2", [128, 2], F32)
        b_der2 = Buf("der2")
        k.op("dve", lambda e: e.memset(der2[:, 0:1], EPS), W=[b_der2])
        b_eps_ap = der2[:, 0:1]

        fst = [sb("fst%d" % i, [128, 12], F32) for i in range(2)]
        b_fst = [[Buf("fst%d_%d" % (i, j)) for j in range(12)] for i in range(2)]
        xq4 = [(xq[0], b_xq[0], "xq0"), (xq[1], b_xq[1], "xq1")]
        fcnt = [0]

        hq4 = []

        def front1(src_d, t0, ntok, par):
            early = len(hq4) >= ntok // 128
            for q in range(ntok // 128):
                xs_, b_xs_, key_ = xq4[q % len(xq4)]
                k.dma("sp", xs_[:], src_d[t0 + q * 128:t0 + (q + 1) * 128, :], W=[b_xs_], key=key_)
                hb, b_hb = hq4[q] if early else (hq[q % 2], b_hq[q % 2])
                k.op("act", lambda e: e.activation(out=hb[:], in_=xs_[:], func=AF.Square, accum_out=fst[par][:, q:q + 1]), R=[b_xs_], W=[b_hb, b_fst[par][q]])
            for q in range(ntok // 128):
                k.op("act", lambda e: e.activation(out=fst[par][:, 4 + q:5 + q], in_=fst[par][:, q:q + 1], func=AF.Sqrt, scale=1.0 / D, bias=b_eps_ap), R=[b_fst[par][q], b_der2], W=[b_fst[par][4 + q]])
                k.op("dve", lambda e: e.reciprocal(out=fst[par][:, 8 + q:9 + q], in_=fst[par][:, 4 + q:5 + q]), R=[b_fst[par][4 + q]], W=[b_fst[par][8 + q]])
            if early:
                for q in range(ntok // 128):
                    xs_, b_xs_, key_ = xq4[q % len(xq4)]
                    hb, b_hb = hq4[q]
                    k.op("pool", lambda e: e.tensor_scalar(out=hb[:], in0=xs_[:], scalar1=fst[par][:, 8 + q:9 + q], scalar2=None, op0=ALU.mult), R=[b_xs_, b_fst[par][8 + q]], W=[b_hb])

        def front2(ntok, hT_t, b_hT_t, par, alloc=None):
            early = len(hq4) >= ntok // 128
            for q in range(ntok // 128):
                xs_, b_xs_, key_ = xq4[q % len(xq4)]
                hb, b_hb = hq4[q] if early else (hq[q % 2], b_hq[q % 2])
                if not early:
                    k.op("dve", lambda e: e.tensor_scalar(out=hb[:], in0=xs_[:], scalar1=fst[par][:, 8 + q:9 + q], scalar2=None, op0=ALU.mult), R=[b_xs_, b_fst[par][8 + q]], W=[b_hb])
                pt, b_pt = (alloc or ps_next)()
                ptb = pt[:].bitcast(BF16)
                for kk in range(8):
                    k.op("pe", lambda e: e.transpose(ptb[:, kk * 128:(kk + 1) * 128], hb[:, kk * 128:(kk + 1) * 128], ident_b), R=[b_hb, b_cstb], W=[b_pt])
                k.op("act", lambda e: e.activation(out=hT_t[:, :, q * 128:(q + 1) * 128], in_=ptb.rearrange("p (k t) -> p k t", k=8), func=AF.Copy), R=[b_pt], W=[b_hT_t])

        def front(src_d, t0, ntok, hT_t, b_hT_t, par=0):
            front1(src_d, t0, ntok, par)
            front2(ntok, hT_t, b_hT_t, par)

        def load_cast(dst_ap, src_ap, b_dst, ncols, scale_ap=None, R=()):
            for c0 in range(0, ncols, 1024):
                n = min(1024, ncols - c0)
                s = stg_n[0] % 2
                stg_n[0] += 1
                k.dma("sp", stg[s][:, 0:n], src_ap[:, c0:c0 + n], W=[b_stg[s]], key="stg%d" % s)
                if scale_ap is None:
                    k.op("act", lambda e: e.activation(out=dst_ap[:, c0:c0 + n], in_=stg[s][:, 0:n], func=AF.Copy), R=[b_stg[s]] + list(R), W=[b_dst])
                else:
                    k.op("dve", lambda e: e.tensor_scalar(out=dst_ap[:, c0:c0 + n], in0=stg[s][:, 0:n], scalar1=scale_ap, scalar2=None, op0=ALU.mult), R=[b_stg[s], b_pp] + list(R), W=[b_dst])

        stg = [sb("stg%d" % i, [128, 1024], F32) for i in range(2)]
        b_stg = [Buf("stg%d" % i) for i in range(2)]
        stg_n = [0]

        TT = min(512, T)
        NT = T // TT
        NQ = TT // 128

        with ExitStack() as es1:
            def sb1(name, shape, dt):
                return es1.enter_context(nc.sbuf_tensor("sb_" + name, shape, dt))
            w1 = sb1("w1", [128, 8, 2048], BF16)
            b_w1 = Buf("w1")
            wo1 = sb1("wo1", [128, 8, 1024], BF16)
            b_wo1 = Buf("wo1")
            wab = sb1("wab", [128, 2, 8, 128], BF16)
            b_wab = Buf("wab")
            for kk in range(8):
                load_cast(w1[:, kk, :], win_l_d[:, kk, :], b_w1, 2048, scale_ap=pp[:, G1 + kk:G1 + kk + 1])
            for kk in range(8):
                load_cast(wo1[:, kk, :], wout_l_d[:, kk, :], b_wo1, 1024, scale_ap=pp[:, GL + kk:GL + kk + 1])
            load_cast(wab[:, 0, :, :].rearrange("p a b -> p (a b)"), wa_d.rearrange("p a b -> p (a b)"), b_wab, 1024)
            load_cast(wab[:, 1, :, :].rearrange("p a b -> p (a b)"), wx_d.rearrange("p a b -> p (a b)"), b_wab, 1024)

            xqa = [sb1("xqa%d" % i, [128, D], F32) for i in range(2)]
            b_xqa = [Buf("xqa%d" % i) for i in range(2)]
            xq4[:] = [(xq[0], b_xq[0], "xq0"), (xq[1], b_xq[1], "xq1"), (xqa[0], b_xqa[0], "xqa0"), (xqa[1], b_xqa[1], "xqa1")]
            hqa = [sb1("hqa%d" % i, [128, D], BF16) for i in range(2)]
            b_hqa = [Buf("hqa%d" % i) for i in range(2)]
            hq4[:] = [(hq[0], b_hq[0]), (hq[1], b_hq[1]), (hqa[0], b_hqa[0]), (hqa[1], b_hqa[1])]
            NB = 4
            cx = [sb1("cx%d" % i, [128, TT + 3], F32) for i in range(NB)]
            lx = [sb1("lx%d" % i, [128, TT], F32) for i in range(NB)]
            lxb = [sb1("lxb%d" % i, [128, TT], BF16) for i in range(NB)]
            rr = [sb1("rr%d" % i, [128, TT], F32) for i in range(NB)]
            ii = [sb1("ii%d" % i, [128, TT], F32) for i in range(NB)]
            a2 = [sb1("a2%d" % i, [128, TT], F32) for i in range(NB)]
            hs = [sb1("hs%d" % i, [128, TT], F32) for i in range(NB)]
            gg = [sb1("gg%d" % i, [128, TT], F32) for i in range(NB)]
            b_cx = [Buf("cx") for _ in range(NB)]
            b_lx = [Buf("lx") for _ in range(NB)]
            b_lxb = [Buf("lxb") for _ in range(NB)]
            b_rr = [Buf("rr") for _ in range(NB)]
            b_ii = [Buf("ii") for _ in range(NB)]
            b_a2 = [Buf("a2") for _ in range(NB)]
            b_hs = [Buf("hs") for _ in range(NB)]
            b_gg = [Buf("gg") for _ in range(NB)]
            hist = sb1("hist", [128, 8, 3], F32)
            b_hist = [Buf("hist%d" % i) for i in range(8)]
            carry = sb1("carry", [128, 8], F32)
            b_carry = [Buf("carry%d" % i) for i in range(8)]
            k.op("dve", lambda e: e.memset(hist[:], 0.0), W=b_hist)
            k.op("dve", lambda e: e.memset(carry[:], 0.0), W=b_carry)
            yT = sb1("yT1", [128, 8, TT], BF16)
            b_yT = [Buf("yT1_%d" % i) for i in range(8)]
            sq = sb1("sq1", [128, 8, TT], BF16)
            b_sq = [Buf("sq1_%d" % i) for i in range(8)]
            po = [sb1("po%d" % i, [128, D], F32) for i in range(2)]
            b_po = [Buf("po%d" % i) for i in range(2)]
            pon = 0
            blk = 0
            CAST_ENG = "pool"
            for it in range(NT if 1 in sweeps else 0):
                t0 = it * TT
                hs_i = it % 2
                if it == 0:
                    front(x_d, t0, TT, hT[hs_i], b_hT[hs_i], par=0)
                hTt, b_hTt = hT[hs_i], b_hT[hs_i]
                def stA(pr_, hTx, b_hTx, slots):
                    bs = [(2 * pr_ + j, slots[j]) for j in range(2)]
                    for b, s in bs:
                        px, b_px = ps_next()
                        for kk in range(8):
                            mm(px[:, 0:TT], w1[:, kk, b * 128:(b + 1) * 128], hTx[:, kk, 0:TT], kk == 0, kk == 7, [b_w1, b_hTx], [b_px])
                        k.op("dve", lambda e: e.tensor_copy(out=cx[s][:, 0:3], in_=hist[:, b, :]), R=[b_hist[b]], W=[b_cx[s]])
                        k.op("act", lambda e: e.activation(out=cx[s][:, 3:3 + TT], in_=px[:, 0:TT], func=AF.Copy), R=[b_px], W=[b_cx[s]])
                        k.op("dve", lambda e: e.tensor_copy(out=hist[:, b, :], in_=cx[s][:, TT:TT + 3]), R=[b_cx[s]], W=[b_hist[b]])
                        cw = lambda j: pp[:, LCW + b * 4 + j:LCW + b * 4 + j + 1]
                        k.op("dve", lambda e: e.tensor_scalar(out=lx[s][:], in0=cx[s][:, 0:TT], scalar1=cw(0), scalar2=pp[:, LCB + b:LCB + b + 1], op0=ALU.mult, op1=ALU.add), R=[b_cx[s], b_pp], W=[b_lx[s]])
                        for j in range(1, 4):
                            k.op("dve", lambda e: e.scalar_tensor_tensor(out=lx[s][:], in0=cx[s][:, j:j + TT], scalar=cw(j), in1=lx[s][:], op0=ALU.mult, op1=ALU.add), R=[b_cx[s], b_pp, b_lx[s]], W=[b_lx[s]])
                        k.op("dve", lambda e: e.tensor_copy(out=lxb[s][:], in_=lx[s][:]), R=[b_lx[s]], W=[b_lxb[s]])

                def stB(pr_, hTx, b_hTx, slots):
                    bs = [(2 * pr_ + j, slots[j]) for j in range(2)]
                    prs, pis = {}, {}
                    for b, s in bs:
                        pr, b_pr = ps_next()
                        mm(pr[:, 0:TT], wab[:, 0, b, :], lxb[s][:], True, True, [b_wab, b_lxb[s]], [b_pr])
                        pi, b_pi = ps_next()
                        mm(pi[:, 0:TT], wab[:, 1, b, :], lxb[s][:], True, True, [b_wab, b_lxb[s]], [b_pi])
                        prs[b] = (pr, b_pr)
                        pis[b] = (pi, b_pi)
                    for b, s in bs:
                        pr, b_pr = prs[b]
                        pi, b_pi = pis[b]
                        k.op("act", lambda e: e.activation(out=rr[s][:], in_=pr[:, 0:TT], func=AF.Sigmoid, bias=pp[:, BA + b:BA + b + 1]), R=[b_pr, b_pp], W=[b_rr[s]])
                        k.op("act", lambda e: e.activation(out=ii[s][:], in_=pi[:, 0:TT], func=AF.Sigmoid, bias=pp[:, BX + b:BX + b + 1]), R=[b_pi, b_pp], W=[b_ii[s]])
                        k.op("pool", lambda e: e.tensor_tensor(out=ii[s][:], in0=ii[s][:], in1=lx[s][:], op=ALU.mult), R=[b_ii[s], b_lx[s]], W=[b_ii[s]])
                    for b, s in bs:
                        k.op("act", lambda e: e.activation(out=a2[s][:], in_=rr[s][:], func=AF.Exp, scale=der[:, 8 + b:9 + b]), R=[b_rr[s], b_der], W=[b_a2[s]])
                        k.op("act", lambda e: e.activation(out=rr[s][:], in_=rr[s][:], func=AF.Exp, scale=der[:, b:b + 1]), R=[b_rr[s], b_der], W=[b_rr[s]])
                    for b, s in bs:
                        k.op("act", lambda e: e.activation(out=a2[s][:], in_=a2[s][:], func=AF.Sqrt, scale=-1.0, bias=1.0), R=[b_a2[s]], W=[b_a2[s]])
                    for b, s in bs:
                        k.op("dve", lambda e: e.tensor_tensor(out=ii[s][:], in0=ii[s][:], in1=a2[s][:], op=ALU.mult), R=[b_ii[s], b_a2[s]], W=[b_ii[s]])
                        k.op("dve", lambda e: e.tensor_tensor_scan(out=hs[s][:], data0=rr[s][:], data1=ii[s][:], initial=carry[:, b:b + 1], op0=ALU.mult, op1=ALU.add), R=[b_rr[s], b_ii[s], b_carry[b]], W=[b_hs[s]])
                        k.op("dve", lambda e: e.tensor_copy(out=carry[:, b:b + 1], in_=hs[s][:, TT - 1:TT]), R=[b_hs[s]], W=[b_carry[b]])
                    pgs = {}
                    for b, s in bs:
                        pg, b_pg = ps_next()
                        for kk in range(8):
                            mm(pg[:, 0:TT], w1[:, kk, 1024 + b * 128:1024 + (b + 1) * 128], hTx[:, kk, 0:TT], kk == 0, kk == 7, [b_w1, b_hTx], [b_pg])
                        pgs[b] = (pg, b_pg)
                    for b, s in bs:
                        pg, b_pg = pgs[b]
                        k.op("act", lambda e: e.activation(out=gg[s][:], in_=pg[:, 0:TT], func=AF.Gelu_apprx_tanh), R=[b_pg], W=[b_gg[s]])
                    for b, s in bs:
                        k.op("pool", lambda e: e.tensor_tensor(out=gg[s][:], in0=gg[s][:], in1=hs[s][:], op=ALU.mult), R=[b_gg[s], b_hs[s]], W=[b_gg[s]])
                        k.op(CAST_ENG, lambda e: e.tensor_tensor(out=sq[:, b, :], in0=gg[s][:], in1=gg[s][:], op=ALU.mult), R=[b_gg[s]], W=[b_sq[b]])
                        k.op(CAST_ENG, lambda e: e.tensor_copy(out=yT[:, b, :], in_=gg[s][:]), R=[b_gg[s]], W=[b_yT[b]])

                slot_of = lambda pr_: [(2 * (pr_ % 2)) % NB, (2 * (pr_ % 2) + 1) % NB]
                if it == 0:
                    stA(0, hTt, b_hTt, slot_of(0))
                for pr_ in range(4):
                    if pr_ == 0 and it + 1 < NT:
                        front1(x_d, t0 + TT, TT, (it + 1) % 2)
                    if pr_ == 2 and it + 1 < NT:
                        front2(TT, hT[1 - hs_i], b_hT[1 - hs_i], (it + 1) % 2)
                    if pr_ + 1 < 4:
                        stA(pr_ + 1, hTt, b_hTt, slot_of(pr_ + 1))
                    elif it + 1 < NT:
                        stA(0, hT[1 - hs_i], b_hT[1 - hs_i], slot_of(0))
                    stB(pr_, hTt, b_hTt, slot_of(pr_))
                for q in range(NQ):
                    pss, b_pss = ps_next()
                    for b in range(8):
                        mm(pss[:, 0:2], sq[:, b, q * 128:(q + 1) * 128], ones_b[:, 0:2], b == 0, b == 7, [b_sq[b], b_cstb], [b_pss])
                    r_ap, b_r = rstd_of(pss[:, 0:1], b_pss, D)
                    o = pon % 2
                    pon += 1
                    for hf in range(2):
                        pm, b_pm = ps_next()
                        for b in range(8):
                            mm(pm[:, :], yT[:, b, q * 128:(q + 1) * 128], wo1[:, b, hf * 512:(hf + 1) * 512], b == 0, b == 7, [b_yT[b], b_wo1], [b_pm])
                        k.op("act", lambda e: e.activation(out=po[o][:, hf * 512:(hf + 1) * 512], in_=pm[:, :], func=AF.Copy, scale=r_ap), R=[b_pm, b_r], W=[b_po[o]])
                    k.dma("sp", pl_d[t0 + q * 128:t0 + (q + 1) * 128, :], po[o][:], R=[b_po[o]], key="po%d" % o, store=True)
            k.barrier()


        junk = sb("junk", [128, 512], BF16)
        b_junk = Buf("junk")

        def rstd_of2(ssa, b_a, ssb, b_b, n):
            t_ap, b_t = st_next()
            k.op("dve", lambda e: e.tensor_tensor(out=t_ap, in0=ssa, in1=ssb, op=ALU.add), R=[b_a, b_b], W=[b_t])
            return rstd_of(t_ap, b_t, n)

        with ExitStack() as es2:
            def sb2(name, shape, dt):
                return es2.enter_context(nc.sbuf_tensor("sb_" + name, shape, dt))
            w2 = sb2("w2", [128, 8, 2576], BF16)
            b_w2 = Buf("w2")
            wo2 = sb2("wo2", [128, 8, 1024], BF16)
            b_wo2 = Buf("wo2")
            for kk in range(8):
                load_cast(w2[:, kk, :], win_s_d[:, kk, :], b_w2, 2576, scale_ap=pp[:, G1 + kk:G1 + kk + 1])
            for kk in range(8):
                load_cast(wo2[:, kk, :], wout_s_d[:, kk, :], b_wo2, 1024, scale_ap=pp[:, GS + kk:GS + kk + 1])
            cx2 = [sb2("cx2%d" % i, [128, TT + 4], BF16) for i in range(3)]
            dgt = sb2("dgt", [128, 48, 128], BF16)
            b_dgt = Buf("dgt")
            for m_ in range(48):
                k.op("dve", lambda e: e.tensor_scalar(out=dgt[:, m_, :], in0=ident_b, scalar1=pp[:, SCW + m_:SCW + m_ + 1], scalar2=None, op0=ALU.mult), R=[b_cstb, b_pp], W=[b_dgt])
            cv = [sb2("cv%d" % i, [128, TT], F32) for i in range(3)]
            b_cx2 = [Buf("cx2") for _ in range(3)]
            b_cv = [Buf("cv") for _ in range(3)]
            hist2 = sb2("hist2", [128, 12, 3], F32)
            b_hist2 = [Buf("hist2_%d" % i) for i in range(12)]
            k.op("dve", lambda e: e.memset(hist2[:], 0.0), W=b_hist2)
            sxT = sb2("sxT", [128, NQ, 1024], F32)
            b_sxT = [Buf("sxT%d" % i) for i in range(NQ)]
            BCb = sb2("BCb", [128, 4, TT], BF16)
            b_BC = [Buf("BC%d" % i) for i in range(4)]
            Btok = sb2("Btok", [128, NQ, 2, 128], BF16)
            b_Btok = [Buf("Btok%d" % i) for i in range(2)]
            zs = [sb2("zs%d" % i, [128, 1024], F32) for i in range(2)]
            b_zs = [Buf("zs%d" % i) for i in range(2)]
            sm = sb2("sm", [128, 8, 64], F32)
            SM_DTR, SM_DT, SM_A, SM_ACS, SM_EACS, SM_DCH, SM_DEC, SM_DTDEC = range(8)
            b_sm = [Buf("sm%d" % i) for i in range(8)]
            scM_ = [sb2("scM%d" % i, [128, 2, 128], BF16) for i in range(2)]
            b_scM_ = [Buf("scM") for _ in range(2)]
            RHSU_ENG = "pool"
            XDEC_ENG = "dve"
            rhsU_ = [sb2("rhsU%d" % i, [128, 8, 128], F32) for i in range(1)] * 2
            b_rhsU_ = [Buf("rhsU")] * 2
            Lb_ = [sb2("Lb%d" % i, [128, 8, 128], BF16) for i in range(2)]
            b_L_ = [Buf("L") for _ in range(2)]
            MT_ = [sb2("MT%d" % i, [128, 8, 128], BF16) for i in range(2)]
            b_MT_ = [Buf("MT") for _ in range(2)]
            xs_ = [sb2("xs%d" % i, [128, 512], BF16) for i in range(2)]
            b_xs_ = [Buf("xs") for _ in range(2)]
            xdec_ = [sb2("xdec%d" % i, [128, 512], BF16) for i in range(2)]
            b_xdec_ = [Buf("xdec") for _ in range(2)]
            xsd_ = [sb2("xsd%d" % i, [128, 512], F32) for i in range(2)]
            b_xsd_ = [Buf("xsd") for _ in range(2)]
            t1 = [sb2("t1%d" % i, [128, 512], F32) for i in range(1)] * 2
            b_t1 = [Buf("t1")] * 2
            ybf = [sb2("ybf%d" % i, [128, 1024], BF16) for i in range(1)] * 2
            b_ybf = [Buf("ybf")] * 2
            Sst = sb2("Sst", [128, 1024], F32)
            b_S = [Buf("S%d" % i) for i in range(2)]
            Sbf = sb2("Sbf", [128, 1024], BF16)
            b_Sbf = [Buf("Sbf%d" % i) for i in range(2)]
            k.op("dve", lambda e: e.memset(Sst[:], 0.0), W=b_S)
            k.op("dve", lambda e: e.memset(Sbf[:], 0.0), W=b_Sbf)
            yT2 = sb2("yT2", [128, 8, TT], BF16)
            b_yT2 = Buf("yT2")
            plq = sb2("plq", [128, 1024], F32)
            b_plq = Buf("plq")
            xr = sb2("xr", [128, 1024], F32)
            b_xr = Buf("xr")
            b_pl_scr = Buf("pl_scr")
            xq4[:] = [(xq[0], b_xq[0], "xq0"), (xq[1], b_xq[1], "xq1"), (stg[0], b_stg[0], "stg0"), (stg[1], b_stg[1], "stg1")]
            hq4[:] = []
            blk = 0
            t1n = 0
            for it in range(NT if 2 in sweeps else 0):
                t0 = it * TT
                hTt, b_hTt = hT[it % 2], b_hT[it % 2]
                if it == 0:
                    front(x_d, t0, TT, hTt, b_hTt, par=0)
                def conv_mm(b):
                    s = b % 3
                    px, b_px = ps_next()
                    c0 = 1024 + b * 128
                    for kk in range(8):
                        mm(px[:, 0:TT], w2[:, kk, c0:c0 + 128], hTt[:, kk, 0:TT], kk == 0, kk == 7, [b_w2, b_hTt], [b_px])
                    k.op("dve", lambda e: e.tensor_copy(out=cx2[s][:, 0:3], in_=hist2[:, b, :]), R=[b_hist2[b]], W=[b_cx2[s]])
                    k.op("act", lambda e: e.activation(out=cx2[s][:, 3:3 + TT], in_=px[:, 0:TT], func=AF.Copy), R=[b_px], W=[b_cx2[s]])
                    k.op("dve", lambda e: e.tensor_copy(out=hist2[:, b, :], in_=cx2[s][:, TT:TT + 3]), R=[b_cx2[s]], W=[b_hist2[b]])
                    pc, b_pc = ps_next()
                    for j in range(4):
                        mm(pc[:, 0:TT], dgt[:, b * 4 + j, :], cx2[s][:, j:j + TT], j == 0, j == 3, [b_dgt, b_cx2[s]], [b_pc])
                    if b < 8:
                        k.op("act", lambda e: e.activation(out=cv[s][:], in_=pc[:, 0:TT], func=AF.Silu, bias=pp[:, SCB + b:SCB + b + 1]), R=[b_pc, b_pp], W=[b_cv[s]])
                    else:
                        j4 = b - 8
                        k.op("act", lambda e: e.activation(out=BCb[:, j4, :], in_=pc[:, 0:TT], func=AF.Silu, bias=pp[:, SCB + b:SCB + b + 1]), R=[b_pc, b_pp], W=[b_BC[j4]])

                def conv_tr(b):
                    s = b % 3
                    if b < 8:
                        pt, b_pt = ps_next()
                        for q in range(NQ):
                            k.op("pe", lambda e: e.transpose(pt[:, q * 128:(q + 1) * 128], cv[s][:, q * 128:(q + 1) * 128], ident_f), R=[b_cv[s], b_cst], W=[b_pt])
                        k.op("act", lambda e: e.activation(out=sxT[:, :, b * 128:(b + 1) * 128], in_=pt[:, 0:NQ * 128].rearrange("p (q c) -> p q c", q=NQ), func=AF.Copy), R=[b_pt], W=b_sxT)
                    elif b < 10:
                        j4 = b - 8
                        pt, b_pt = ps_next()
                        ptb = pt[:].bitcast(BF16)
                        for q in range(NQ):
                            k.op("pe", lambda e: e.transpose(ptb[:, q * 128:(q + 1) * 128], BCb[:, j4, q * 128:(q + 1) * 128], ident_b), R=[b_BC[j4], b_cstb], W=[b_pt])
                        k.op("act", lambda e: e.activation(out=Btok[:, :, j4, :], in_=ptb[:, 0:NQ * 128].rearrange("p (q c) -> p q c", q=NQ), func=AF.Copy), R=[b_pt], W=[b_Btok[j4]])

                for b in range(12 + 2):
                    if b < 12:
                        conv_mm(b)
                    if b >= 2:
                        conv_tr(b - 2)
                pd, b_pd = ps_next()
                for q in range(NQ):
                    for kk in range(8):
                        mm(pd[:, q * 16:(q + 1) * 16], hTt[:, kk, q * 128:(q + 1) * 128], w2[:, kk, 2560:2576], kk == 0, kk == 7, [b_w2, b_hTt], [b_pd])
                NH = NQ * 16
                bq = lambda ap16: ap16.unsqueeze(1).to_broadcast([128, NQ, 16])
                v3 = lambda i: sm[:, i, 0:NH].rearrange("p (q h) -> p q h", q=NQ)
                k.op("dve", lambda e: e.tensor_tensor(out=v3(SM_DTR), in0=pd[:, 0:NH].rearrange("p (q h) -> p q h", q=NQ), in1=bq(bc[:, DTB:DTB + 16]), op=ALU.add), R=[b_pd, b_bc], W=[b_sm[SM_DTR]])
                k.op("act", lambda e: e.activation(out=sm[:, SM_DTR, 0:NH], in_=sm[:, SM_DTR, 0:NH], func=AF.Exp), R=[b_sm[SM_DTR]], W=[b_sm[SM_DTR]])
                k.op("act", lambda e: e.activation(out=sm[:, SM_DT, 0:NH], in_=sm[:, SM_DTR, 0:NH], func=AF.Ln, bias=1.0), R=[b_sm[SM_DTR]], W=[b_sm[SM_DT]])
                k.op("dve", lambda e: e.tensor_tensor(out=v3(SM_A), in0=v3(SM_DT), in1=bq(der[:, 16:32]), op=ALU.mult), R=[b_sm[SM_DT], b_der], W=[b_sm[SM_A]])
                pc, b_pc = ps_next()
                mm(pc[:, 0:NH], U_f, sm[:, SM_A, 0:NH], True, True, [b_cst, b_sm[SM_A]], [b_pc])
                mm(pc[:, 64:64 + NH], ones_f, sm[:, SM_A, 0:NH], True, True, [b_cst, b_sm[SM_A]], [b_pc])
                k.op("act", lambda e: e.activation(out=sm[:, SM_ACS, 0:NH], in_=pc[:, 0:NH], func=AF.Copy), R=[b_pc], W=[b_sm[SM_ACS]])
                k.op("act", lambda e: e.activation(out=sm[:, SM_EACS, 0:NH], in_=pc[:, 0:NH], func=AF.Exp), R=[b_pc], W=[b_sm[SM_EACS]])
                k.op("act", lambda e: e.activation(out=sm[:, SM_DCH, 0:NH], in_=pc[:, 64:64 + NH], func=AF.Exp), R=[b_pc], W=[b_sm[SM_DCH]])
                k.op("dve", lambda e: e.tensor_tensor(out=sm[:, SM_DEC, 0:NH], in0=pc[:, 64:64 + NH], in1=sm[:, SM_ACS, 0:NH], op=ALU.subtract), R=[b_pc, b_sm[SM_ACS]], W=[b_sm[SM_DEC]])
                k.op("act", lambda e: e.activation(out=sm[:, SM_DEC, 0:NH], in_=sm[:, SM_DEC, 0:NH], func=AF.Exp), R=[b_sm[SM_DEC]], W=[b_sm[SM_DEC]])
                k.op("dve", lambda e: e.tensor_tensor(out=sm[:, SM_DTDEC, 0:NH], in0=sm[:, SM_DEC, 0:NH], in1=sm[:, SM_DT, 0:NH], op=ALU.mult), R=[b_sm[SM_DEC], b_sm[SM_DT]], W=[b_sm[SM_DTDEC]])
                items = [(q, g) for q in range(NQ) for g in range(2)]
                NI = len(items)
                held = {}
                ssq_all = {}

                def St0(n):
                    q, g = items[n]
                    qs = slice(q * 128, (q + 1) * 128)
                    if g == 0:
                        zsl = q % 2
                        for hf in range(2):
                            pz, b_pz = ps_next()
                            for kk in range(8):
                                mm(pz[:, :], hTt[:, kk, qs], w2[:, kk, hf * 512:(hf + 1) * 512], kk == 0, kk == 7, [b_hTt, b_w2], [b_pz])
                            k.op("act", lambda e: e.activation(out=zs[zsl][:, hf * 512:(hf + 1) * 512], in_=pz[:, :], func=AF.Silu), R=[b_pz], W=[b_zs[zsl]])
                        psc, b_psc = ps_next()
                        for g2 in range(2):
                            mm(psc[:, g2 * 128:(g2 + 1) * 128], BCb[:, g2, qs], BCb[:, 2 + g2, qs], True, True, [b_BC[g2], b_BC[2 + g2]], [b_psc])
                        k.op("dve", lambda e: e.tensor_tensor(out=scM_[q % 2][:], in0=psc[:, 0:256].rearrange("p (g l) -> p g l", g=2), in1=U_f.unsqueeze(1).to_broadcast([128, 2, 128]), op=ALU.mult), R=[b_psc, b_cst], W=[b_scM_[q % 2]])
                    hc = q * 16 + g * 8
                    sl = n % 2
                    k.op(RHSU_ENG, lambda e: e.tensor_tensor(out=rhsU_[sl][:], in0=U_f.unsqueeze(1).to_broadcast([128, 8, 128]), in1=sm[:, SM_A, hc:hc + 8].unsqueeze(2).to_broadcast([128, 8, 128]), op=ALU.mult), R=[b_cst, b_sm[SM_A]], W=[b_rhsU_[sl]])
                    for hh in range(2):
                        pL, b_pL = ps_next()
                        mm(pL[:, :], SL_f, rhsU_[sl][:, hh * 4:(hh + 1) * 4, :].rearrange("p h l -> p (h l)"), True, True, [b_cst, b_rhsU_[sl]], [b_pL])
                        k.op("act", lambda e: e.activation(out=Lb_[sl][:, hh * 4:(hh + 1) * 4, :].rearrange("p h l -> p (h l)"), in_=pL[:, :], func=AF.Exp), R=[b_pL], W=[b_L_[sl]])

                def St1(n):
                    q, g = items[n]
                    qs = slice(q * 128, (q + 1) * 128)
                    gsl = slice(g * 512, (g + 1) * 512)
                    hc = q * 16 + g * 8
                    sl = n % 2
                    colb = lambda i, nn: sm[:, i, hc:hc + 8].unsqueeze(2).to_broadcast([128, 8, nn])
                    MT, b_MT, xs, b_xs, xdec, b_xdec = MT_[sl], b_MT_[sl], xs_[sl], b_xs_[sl], xdec_[sl], b_xdec_[sl]
                    k.op("dve", lambda e: e.tensor_tensor(out=MT[:], in0=Lb_[sl][:], in1=scM_[q % 2][:, g, :].unsqueeze(1).to_broadcast([128, 8, 128]), op=ALU.mult), R=[b_L_[sl], b_scM_[q % 2]], W=[b_MT])
                    sx3 = sxT[:, q, gsl].rearrange("p (h c) -> p h c", h=8)
                    k.op(XDEC_ENG, lambda e: e.tensor_tensor(out=xs[:].rearrange("p (h c) -> p h c", h=8), in0=sx3, in1=colb(SM_DT, 64), op=ALU.mult), R=[b_sxT[q], b_sm[SM_DT]], W=[b_xs])
                    k.op(XDEC_ENG, lambda e: e.tensor_tensor(out=xdec[:].rearrange("p (h c) -> p h c", h=8), in0=sx3, in1=colb(SM_DTDEC, 64), op=ALU.mult), R=[b_sxT[q], b_sm[SM_DTDEC]], W=[b_xdec])
                    dsk = bc[:, DSK + g * 8:DSK + g * 8 + 8].unsqueeze(2).to_broadcast([128, 8, 64])
                    k.op(XDEC_ENG, lambda e: e.tensor_tensor(out=xsd_[sl][:].rearrange("p (h c) -> p h c", h=8), in0=sx3, in1=dsk, op=ALU.mult), R=[b_sxT[q], b_bc], W=[b_xsd_[sl]])
                    pyd, b_pyd, i1 = ps_hold()
                    for h in range(8):
                        mm(pyd[:, h * 64:(h + 1) * 64], MT[:, h, :], xs[:, h * 64:(h + 1) * 64], True, True, [b_MT, b_xs], [b_pyd])
                    pyo, b_pyo, i2 = ps_hold()
                    mm(pyo[:, :], BCb[:, 2 + g, qs], Sbf[:, gsl], True, True, [b_BC[2 + g], b_Sbf[g]], [b_pyo])
                    pst, b_pst, i3 = ps_hold()
                    mm(pst[:, :], Btok[:, q, g, :], xdec[:], True, True, [b_Btok[g], b_xdec], [b_pst])
                    held[n] = (pyd, b_pyd, i1, pyo, b_pyo, i2, pst, b_pst, i3)

                def St2(n):
                    q, g = items[n]
                    qs = slice(q * 128, (q + 1) * 128)
                    gsl = slice(g * 512, (g + 1) * 512)
                    hc = q * 16 + g * 8
                    sl = n % 2
                    zsl = q % 2
                    yb = q % 2
                    colb = lambda i, nn: sm[:, i, hc:hc + 8].unsqueeze(2).to_broadcast([128, 8, nn])
                    pyd, b_pyd, i1, pyo, b_pyo, i2, pst, b_pst, i3 = held.pop(n)
                    ts_ = n % 2
                    tt3 = t1[ts_][:].rearrange("p (h c) -> p h c", h=8)
                    k.op("dve", lambda e: e.tensor_tensor(out=tt3, in0=pyo[:, :].rearrange("p (h c) -> p h c", h=8), in1=colb(SM_EACS, 64), op=ALU.mult), R=[b_pyo, b_sm[SM_EACS]], W=[b_t1[ts_]])
                    k.op("dve", lambda e: e.tensor_tensor(out=t1[ts_][:], in0=t1[ts_][:], in1=pyd[:, :], op=ALU.add), R=[b_t1[ts_], b_pyd], W=[b_t1[ts_]])
                    k.op("dve", lambda e: e.tensor_tensor(out=t1[ts_][:], in0=t1[ts_][:], in1=xsd_[sl][:], op=ALU.add), R=[b_t1[ts_], b_xsd_[sl]], W=[b_t1[ts_]])
                    k.op("dve", lambda e: e.tensor_tensor(out=t1[ts_][:], in0=t1[ts_][:], in1=zs[zsl][:, gsl], op=ALU.mult), R=[b_t1[ts_], b_zs[zsl]], W=[b_t1[ts_]])
                    ss_ap, b_ss = st_next()
                    k.op("act", lambda e: e.activation(out=junk[:], in_=t1[ts_][:], func=AF.Square, accum_out=ss_ap), R=[b_t1[ts_]], W=[b_junk, b_ss])
                    ssq_all[(q, g)] = (ss_ap, b_ss)
                    k.op("act", lambda e: e.activation(out=ybf[yb][:, gsl], in_=t1[ts_][:], func=AF.Copy), R=[b_t1[ts_]], W=[b_ybf[yb]])
                    S3 = Sst[:, gsl].rearrange("p (h c) -> p h c", h=8)
                    k.op(XDEC_ENG, lambda e: e.tensor_tensor(out=S3, in0=S3, in1=colb(SM_DCH, 64), op=ALU.mult), R=[b_S[g], b_sm[SM_DCH]], W=[b_S[g]])
                    k.op("dve", lambda e: e.tensor_tensor(out=Sst[:, gsl], in0=Sst[:, gsl], in1=pst[:, :], op=ALU.add), R=[b_S[g], b_pst], W=[b_S[g]])
                    k.op("act", lambda e: e.activation(out=Sbf[:, gsl], in_=Sst[:, gsl], func=AF.Copy), R=[b_S[g]], W=[b_Sbf[g]])
                    ps_release(i1)
                    ps_release(i2)
                    ps_release(i3)
                    if g == 1:
                        St3(q)

                def St3(q):
                    qs = slice(q * 128, (q + 1) * 128)
                    yb = q % 2
                    a0, a1 = ssq_all.pop((q, 0)), ssq_all.pop((q, 1))
                    r_s, b_rs = rstd_of2(a0[0], a0[1], a1[0], a1[1], D)
                    pt, b_pt = ps_next()
                    ptb = pt[:].bitcast(BF16)
                    for kk in range(8):
                        k.op("pe", lambda e: e.transpose(ptb[:, kk * 128:(kk + 1) * 128], ybf[yb][:, kk * 128:(kk + 1) * 128], ident_b), R=[b_ybf[yb], b_cstb], W=[b_pt])
                    k.op("act", lambda e: e.activation(out=yT2[:, :, qs], in_=ptb.rearrange("p (k t) -> p k t", k=8), func=AF.Copy), R=[b_pt], W=[b_yT2])
                    k.dma("sp", plq[:], pl_d[t0 + q * 128:t0 + (q + 1) * 128, :], W=[b_plq], key="plq")
                    k.dma("sp", xr[:], x_d[t0 + q * 128:t0 + (q + 1) * 128, :], W=[b_xr], key="xr")
                    ssm = []
                    for hf in range(2):
                        hsl = slice(hf * 512, (hf + 1) * 512)
                        pm, b_pm = ps_next()
                        for b in range(8):
                            mm(pm[:, :], yT2[:, b, qs], wo2[:, b, hsl], b == 0, b == 7, [b_yT2, b_wo2], [b_pm])
                        k.op("dve", lambda e: e.scalar_tensor_tensor(out=plq[:, hsl], in0=pm[:, :], scalar=r_s, in1=plq[:, hsl], op0=ALU.mult, op1=ALU.add), R=[b_pm, b_rs, b_plq], W=[b_plq])
                        ss_ap, b_ss = st_next()
                        k.op("act", lambda e: e.activation(out=junk[:], in_=plq[:, hsl], func=AF.Square, accum_out=ss_ap), R=[b_plq], W=[b_junk, b_ss])
                        ssm.append((ss_ap, b_ss))
                    r_p, b_rp = rstd_of2(ssm[0][0], ssm[0][1], ssm[1][0], ssm[1][1], D)
                    k.op("dve", lambda e: e.scalar_tensor_tensor(out=plq[:], in0=plq[:], scalar=r_p, in1=bc[:, GP1:GP1 + 1024], op0=ALU.mult, op1=ALU.mult), R=[b_plq, b_rp, b_bc], W=[b_plq])
                    k.op("dve", lambda e: e.tensor_tensor(out=xr[:], in0=xr[:], in1=plq[:], op=ALU.add), R=[b_xr, b_plq], W=[b_xr])
                    k.dma("sp", x1_d[t0 + q * 128:t0 + (q + 1) * 128, :], xr[:], R=[b_xr], key="xr_st", store=True)

                for step in range(NI + 2):
                    if step == 0 and it + 1 < NT:
                        front1(x_d, t0 + TT, TT, (it + 1) % 2)
                    if step == 6 and it + 1 < NT:
                        front2(TT, hT[(it + 1) % 2], b_hT[(it + 1) % 2], (it + 1) % 2)
                    if 2 <= step:
                        St2(step - 2)
                    if step < NI:
                        St0(step)
                    if 1 <= step < NI + 1:
                        St1(step - 1)
            k.barrier()

        with ExitStack() as es3:
            def sb3(name, shape, dt):
                return es3.enter_context(nc.sbuf_tensor("sb_" + name, shape, dt))
            wdn = sb3("wdn", [128, NF, 1024], BF16)
            b_wdn = Buf("wdn")
            for f in range(NF):
                load_cast(wdn[:, f, :], wdn_d[:, f, :], b_wdn, 1024)
            wgc = [sb3("wgc%d" % i, [128, 8, 256], BF16) for i in range(2)]
            b_wgc = [Buf("wgc%d" % i) for i in range(2)]
            b_wgs = [Buf("wgu_scr%d" % i) for i in range(NF)]
            for f in range(NF):
                c2 = f % 2
                for kh in range(2):
                    s = stg_n[0] % 2
                    stg_n[0] += 1
                    k.dma("sp", stg[s][:, 0:1024].rearrange("p (k c) -> p k c", k=4), wgu_d[f, :, kh * 4:(kh + 1) * 4, :], W=[b_stg[s]], key="stg%d" % s)
                    for k4 in range(4):
                        kk = kh * 4 + k4
                        k.op("dve" if k4 % 2 == 0 else "act",
                             (lambda e: e.tensor_scalar(out=wgc[c2][:, kk, :], in0=stg[s][:, k4 * 256:(k4 + 1) * 256], scalar1=pp[:, G2 + kk:G2 + kk + 1], scalar2=None, op0=ALU.mult)) if k4 % 2 == 0 else
                             (lambda e: e.activation(out=wgc[c2][:, kk, :], in_=stg[s][:, k4 * 256:(k4 + 1) * 256], func=AF.Copy, scale=pp[:, G2 + kk:G2 + kk + 1])),
                             R=[b_stg[s], b_pp], W=[b_wgc[c2]])
                k.dma("sp", wgu_s[f], wgc[c2][:].rearrange("p k c -> p (k c)"), R=[b_wgc[c2]], W=[b_wgs[f]], key="wgs%d" % c2)
            k.barrier()
            xqc = [sb3("xqc%d" % i, [128, D], F32) for i in range(2)]
            b_xqc = [Buf("xqc%d" % i) for i in range(2)]
            xq4[:] = [(xq[0], b_xq[0], "xq0"), (xq[1], b_xq[1], "xq1"), (xqc[0], b_xqc[0], "xqc0"), (xqc[1], b_xqc[1], "xqc1")]
            hqc = [sb3("hqc%d" % i, [128, D], BF16) for i in range(2)]
            b_hqc = [Buf("hqc%d" % i) for i in range(2)]
            hq4[:] = [(hq[0], b_hq[0]), (hq[1], b_hq[1]), (hqc[0], b_hqc[0]), (hqc[1], b_hqc[1])]
            NR = 6
            wr = [sb3("wr%d" % i, [128, 8, 256], BF16) for i in range(NR)]
            b_wr = [Buf("wr%d" % i) for i in range(NR)]
            gT = sb3("gT", [128, NF, TT], BF16)
            b_gT = Buf("gT")
            sg = [sb3("sg%d" % i, [128, TT], F32) for i in range(2)]
            b_sg = [Buf("sg%d" % i) for i in range(2)]
            ob = [sb3("ob%d" % i, [128, 1024], F32) for i in range(2)]
            b_ob = [Buf("ob%d" % i) for i in range(2)]
            xr3 = [sb3("xr3%d" % i, [128, 1024], F32) for i in range(2)]
            b_xr3 = [Buf("xr3%d" % i) for i in range(2)]
            wn = 0
            on = 0
            for it in range(NT if 3 in sweeps else 0):
                t0 = it * TT
                hTt, b_hTt = hT[it % 2], b_hT[it % 2]
                if it == 0:
                    front(x1_d, t0, TT, hTt, b_hTt, par=0)
                for f in range(NF):
                    if f == 0 and it + 1 < NT:
                        front1(x1_d, t0 + TT, TT, (it + 1) % 2)
                    if f == 14 and it + 1 < NT:
                        front2(TT, hT[(it + 1) % 2], b_hT[(it + 1) % 2], (it + 1) % 2)
                    sl = wn % NR
                    wn += 1
                    k.dma("sp", wr[sl][:].rearrange("p k c -> p (k c)"), wgu_s[f], R=[b_wgs[f]], W=[b_wr[sl]], key="wr%d" % sl)
                    pg, b_pg = ps_next()
                    for kk in range(8):
                        mm(pg[:, 0:TT], wr[sl][:, kk, 0:128], hTt[:, kk, 0:TT], kk == 0, kk == 7, [b_wr[sl], b_hTt], [b_pg])
                    pu, b_pu = ps_next()
                    for kk in range(8):
                        mm(pu[:, 0:TT], wr[sl][:, kk, 128:256], hTt[:, kk, 0:TT], kk == 0, kk == 7, [b_wr[sl], b_hTt], [b_pu])
                    s2 = f % 2
                    k.op("act", lambda e: e.activation(out=sg[s2][:], in_=pg[:, 0:TT], func=AF.Silu), R=[b_pg], W=[b_sg[s2]])
                    k.op("dve", lambda e: e.tensor_tensor(out=gT[:, f, :], in0=sg[s2][:], in1=pu[:, 0:TT], op=ALU.mult), R=[b_sg[s2], b_pu], W=[b_gT])
                for q in range(NQ):
                    qs = slice(q * 128, (q + 1) * 128)
                    o = on % 2
                    on += 1
                    k.dma("sp", xr3[o][:], x1_d[t0 + q * 128:t0 + (q + 1) * 128, :], W=[b_xr3[o]], key="xr3%d" % o)
                    pms = []
                    for hf in range(2):
                        hsl = slice(hf * 512, (hf + 1) * 512)
                        pm, b_pm = ps_next()
                        for f in range(NF):
                            mm(pm[:, :], gT[:, f, qs], wdn[:, f, hsl], f == 0, f == NF - 1, [b_gT, b_wdn], [b_pm])
                        ss_ap, b_ss = st_next()
                        k.op("act", lambda e: e.activation(out=junk[:], in_=pm[:, :], func=AF.Square, accum_out=ss_ap), R=[b_pm], W=[b_junk, b_ss])
                        pms.append((pm, b_pm, ss_ap, b_ss))
                    r_p, b_rp = rstd_of2(pms[0][2], pms[0][3], pms[1][2], pms[1][3], D)
                    for hf in range(2):
                        hsl = slice(hf * 512, (hf + 1) * 512)
                        pm, b_pm = pms[hf][0], pms[hf][1]
                        k.op("dve", lambda e: e.scalar_tensor_tensor(out=ob[o][:, hsl], in0=pm[:, :], scalar=r_p, in1=bc[:, GP2 + hf * 512:GP2 + (hf + 1) * 512], op0=ALU.mult, op1=ALU.mult), R=[b_pm, b_rp, b_bc], W=[b_ob[o]])
                    k.op("dve", lambda e: e.tensor_tensor(out=ob[o][:], in0=ob[o][:], in1=xr3[o][:], op=ALU.add), R=[b_ob[o], b_xr3[o]], W=[b_ob[o]])
                    k.dma("sp", out_d[t0 + q * 128:t0 + (q + 1) * 128, :], ob[o][:], R=[b_ob[o]], key="ob%d" % o, store=True)
            k.barrier()

        k._wait("sp", k.stores)
    return nc


def _prep_shared(inp):
    f = np.float32
    g = lambda n: np.ascontiguousarray(np.asarray(inp[n], dtype=f)[0])
    w_in = g("w_in")
    pk = lambda w: np.ascontiguousarray(w.reshape(w.shape[0] // 128, 128, w.shape[1]).transpose(1, 0, 2))
    sh = {}
    sh["w_in_l"] = pk(w_in[:, 0:2048])
    sh["w_in_s"] = pk(w_in[:, 2048:4624])
    w_out = g("w_out")
    sh["w_out_l"] = pk(w_out[0:1024])
    sh["w_out_s"] = pk(w_out[1024:2048])
    wg, wu = pk(g("w_gate")), pk(g("w_up"))
    wgu = np.zeros((NF, 128, 8, 256), f)
    for j in range(NF):
        wgu[j, :, :, 0:128] = wg[:, :, j * 128:(j + 1) * 128]
        wgu[j, :, :, 128:256] = wu[:, :, j * 128:(j + 1) * 128]
    sh["w_gu"] = wgu
    sh["w_down"] = pk(g("w_down"))

    def blockdiag(w):
        m = np.zeros((128, 8, 128), f)
        for b in range(8):
            m[0:64, b, 0:64] = w[2 * b]
            m[64:128, b, 64:128] = w[2 * b + 1]
        return m
    sh["wa"] = blockdiag(g("lru_wa"))
    sh["wx"] = blockdiag(g("lru_wx"))
    cm = lambda v: np.ascontiguousarray(v.reshape(-1, 128).T)
    pp = np.zeros((128, 156), f)
    pp[:, 0:8] = cm(g("pre_mix_norm"))
    pp[:, 8:16] = cm(g("pre_ffn_norm"))
    pp[:, 16:24] = cm(g("lru_out_norm"))
    pp[:, 24:32] = cm(g("ssd_out_norm"))
    lcw = g("lru_conv_w")
    for b in range(8):
        for j in range(4):
            pp[:, 32 + b * 4 + j] = lcw[j, b * 128:(b + 1) * 128]
    pp[:, 64:72] = cm(g("lru_conv_b"))
    pp[:, 72:80] = cm(g("lru_ba"))
    pp[:, 80:88] = cm(g("lru_bx"))
    pp[:, 88:96] = cm(g("lru_lambda"))
    scw = g("ssd_conv_w")
    for b in range(12):
        for j in range(4):
            pp[:, 96 + b * 4 + j] = scw[j, b * 128:(b + 1) * 128]
    pp[:, 144:156] = cm(g("ssd_conv_b"))
    sh["pp"] = pp
    bcv = np.concatenate([g("post_mix_norm"), g("post_ffn_norm"), g("ssd_dt_bias"), g("ssd_a_log"), g("ssd_d")])
    sh["bc"] = np.ascontiguousarray(np.broadcast_to(bcv[None, :], (128, bcv.shape[0]))).astype(f)
    cst = np.zeros((128, 4, 128), f)
    i = np.arange(128)
    cst[:, 0, :] = np.eye(128, dtype=f)
    cst[:, 1, :] = (i[:, None] <= i[None, :])
    cst[:, 2, :] = (i[:, None] > i[None, :])
    cst[:, 3, :] = 1.0
    sh["cst"] = cst
    return sh


def kernel(**inputs):
    x = np.asarray(inputs["x"], dtype=np.float32)
    B, T, _ = x.shape
    sh = _prep_shared(inputs)
    nc = build(T)
    in_maps = []
    for c in range(B):
        m = dict(sh)
        m["x"] = np.ascontiguousarray(x[c])
        in_maps.append(m)
    res = run_bass_kernel_spmd(nc, in_maps, core_ids=list(range(B)))
    return np.stack([np.asarray(r["out"]) for r in res.results], axis=0).astype(np.float32)
```

```python
from contextlib import ExitStack
import numpy as np
import concourse.bass as bass
import concourse.mybir as mybir
from concourse.bass_utils import run_bass_kernel_spmd

F32 = mybir.dt.float32
BF16 = mybir.dt.bfloat16
AF = mybir.ActivationFunctionType
ALU = mybir.AluOpType

D = 1024
DFF = 2816
NF = DFF // 128
EPS = 1e-6
EPOCH = 30000


class Buf:
    __slots__ = ("name", "w", "r", "ps_idx", "ps_live")

    def __init__(self, name):
        self.name = name
        self.w = None
        self.r = {}
        self.ps_idx = None
        self.ps_live = False


class K:
    def __init__(self, nc, es):
        self.nc = nc
        self.es = es
        self.eng = {"pe": nc.tensor, "dve": nc.vector, "act": nc.scalar, "pool": nc.gpsimd, "sp": nc.sync}
        self.cnt = {n: 0 for n in self.eng}
        self.sems = {}
        self.seen = {n: {} for n in self.eng}
        self.dcnt = {}
        self.stores = []
        self.psfree = []

    def _sem(self, key):
        if key not in self.sems:
            self.sems[key] = self.es.enter_context(self.nc.semaphore("s_%s_%d" % key))
        return self.sems[key]

    def _wait(self, en, evs):
        best = {}
        for ev in evs:
            if ev is None:
                continue
            key, val = ev
            if self.seen[en].get(key, 0) >= val:
                continue
            if best.get(key, 0) < val:
                best[key] = val
        for key, val in best.items():
            self.eng[en].wait_ge(self._sem(key), val)
            self.seen[en][key] = val

    def _deps(self, en, R, W):
        evs = []
        for b in R:
            evs.append(b.w)
        for b in W:
            if b.w is not None and not (b.w[0][0] == en and en == "pe"):
                evs.append(b.w)
            for e2, ev in b.r.items():
                if e2 != en:
                    evs.append(ev)
        return evs

    def op(self, en, fn, R=(), W=()):
        self._wait(en, self._deps(en, R, W))
        ins = fn(self.eng[en])
        self.cnt[en] += 1
        c = self.cnt[en]
        ep = (c - 1) // EPOCH
        key = (en, ep)
        val = c - ep * EPOCH
        ins.then_inc(self._sem(key), 1)
        ev = (key, val)
        for b in R:
            b.r[en] = ev
        for b in W:
            b.w = ev
            b.r = {}
        return ev

    def dma(self, q, out, in_, R=(), W=(), key=None, store=False):
        evs = []
        for b in R:
            evs.append(b.w)
        for b in W:
            if b.w is not None and b.w[0][0] != "d_" + key:
                evs.append(b.w)
            for e2, ev in b.r.items():
                evs.append(ev)
        self._wait(q, evs)
        ins = self.eng[q].dma_start(out=out, in_=in_)
        c = self.dcnt.get(key, 0) + 16
        ep = c // EPOCH
        if ep != self.dcnt.get(key, 0) // EPOCH:
            c = ep * EPOCH + 16
        self.dcnt[key] = c
        skey = ("d_" + key, ep)
        val = c - ep * EPOCH
        ins.then_inc(self._sem(skey), 16)
        ev = (skey, val)
        for b in R:
            b.r["d_" + key] = ev
        for b in W:
            b.w = ev
            b.r = {}
        if store:
            self.stores.append(ev)
        return ev

    def barrier(self):
        evs = []
        for n, c in self.cnt.items():
            if c > 0:
                ep = (c - 1) // EPOCH
                evs.append(((n, ep), c - ep * EPOCH))
        for key, c in self.dcnt.items():
            ep = c // EPOCH
            evs.append((("d_" + key, ep), c - ep * EPOCH))
        for n in ("pe", "dve", "act", "sp", "pool"):
            self._wait(n, evs)


def build(T, dbg=False, sweeps=(1, 2, 3)):
    nc = bass.Bass("TRN2", target_bir_lowering=False)
    NCH = T // 128
    x_d = nc.dram_tensor("x", [T, D], F32, kind="ExternalInput").ap()
    win_l_d = nc.dram_tensor("w_in_l", [128, 8, 2048], F32, kind="ExternalInput").ap()
    win_s_d = nc.dram_tensor("w_in_s", [128, 8, 2576], F32, kind="ExternalInput").ap()
    wout_l_d = nc.dram_tensor("w_out_l", [128, 8, 1024], F32, kind="ExternalInput").ap()
    wout_s_d = nc.dram_tensor("w_out_s", [128, 8, 1024], F32, kind="ExternalInput").ap()
    wgu_d = nc.dram_tensor("w_gu", [NF, 128, 8, 256], F32, kind="ExternalInput").ap()
    wdn_d = nc.dram_tensor("w_down", [128, NF, 1024], F32, kind="ExternalInput").ap()
    wa_d = nc.dram_tensor("wa", [128, 8, 128], F32, kind="ExternalInput").ap()
    wx_d = nc.dram_tensor("wx", [128, 8, 128], F32, kind="ExternalInput").ap()
    NPP = 156
    pp_d = nc.dram_tensor("pp", [128, NPP], F32, kind="ExternalInput").ap()
    NBC = 2096
    bc_d = nc.dram_tensor("bc", [128, NBC], F32, kind="ExternalInput").ap()
    cst_d = nc.dram_tensor("cst", [128, 4, 128], F32, kind="ExternalInput").ap()
    out_d = nc.dram_tensor("out", [T, D], F32, kind="ExternalOutput").ap()
    pl_d = nc.dram_tensor("pl_scr", [T, D], F32, kind="ExternalOutput" if dbg else "Internal").ap()
    x1_d = nc.dram_tensor("x1_scr", [T, D], F32, kind="ExternalOutput" if dbg else "Internal").ap()
    wgu_s = nc.dram_tensor("wgu_scr", [NF, 128, 8 * 256], BF16, kind="Internal").ap()

    with ExitStack() as es:
        k = K(nc, es)

        def sb(name, shape, dt):
            return es.enter_context(nc.sbuf_tensor("sb_" + name, shape, dt))

        pp = sb("pp", [128, NPP], F32)
        bc = sb("bc", [128, NBC], F32)
        cst = sb("cst", [128, 4, 128], F32)
        cstb = sb("cstb", [128, 4, 128], BF16)
        der = sb("der", [128, 64], F32)
        b_pp, b_bc, b_cst, b_cstb, b_der = Buf("pp"), Buf("bc"), Buf("cst"), Buf("cstb"), Buf("der")
        k.dma("sp", pp[:], pp_d, W=[b_pp], key="pp")
        k.dma("sp", bc[:], bc_d, W=[b_bc], key="bc")
        k.dma("sp", cst[:], cst_d, W=[b_cst], key="cst")
        k.op("dve", lambda e: e.tensor_copy(out=cstb[:], in_=cst[:]), R=[b_cst], W=[b_cstb])
        ident_f, U_f, SL_f, ones_f = cst[:, 0, :], cst[:, 1, :], cst[:, 2, :], cst[:, 3, :]
        ident_b, ones_b = cstb[:, 0, :], cstb[:, 3, :]
        G1, G2, GL, GS, LCW, LCB, BA, BX, LAM, SCW, SCB = 0, 8, 16, 24, 32, 64, 72, 80, 88, 96, 144
        GP1, GP2, DTB, ALOG, DSK = 0, 1024, 2048, 2064, 2080
        k.op("act", lambda e: e.activation(out=der[:, 32:40], in_=pp[:, LAM:LAM + 8], func=AF.Exp, scale=-1.0), R=[b_pp], W=[b_der])
        k.op("act", lambda e: e.activation(out=der[:, 40:48], in_=der[:, 32:40], func=AF.Ln, bias=1.0), R=[b_der], W=[b_der])
        k.op("dve", lambda e: e.tensor_scalar(out=der[:, 0:8], in0=der[:, 40:48], scalar1=-8.0, scalar2=None, op0=ALU.mult), R=[b_der], W=[b_der])
        k.op("dve", lambda e: e.tensor_scalar(out=der[:, 8:16], in0=der[:, 40:48], scalar1=-16.0, scalar2=None, op0=ALU.mult), R=[b_der], W=[b_der])
        k.op("act", lambda e: e.activation(out=der[:, 48:64], in_=bc[:, ALOG:ALOG + 16], func=AF.Exp), R=[b_bc, b_der], W=[b_der])
        k.op("dve", lambda e: e.tensor_scalar(out=der[:, 16:32], in0=der[:, 48:64], scalar1=-1.0, scalar2=None, op0=ALU.mult), R=[b_der], W=[b_der])

        psb = [es.enter_context(nc.psum_tensor("ps%d" % i, [128, 512], F32)) for i in range(8)]
        psbuf = [Buf("ps%d" % i) for i in range(8)]
        psn = [0]

        k.psfree = list(range(8))

        def ps_next():
            i = k.psfree.pop(0)
            k.psfree.append(i)
            return psb[i], psbuf[i]

        def ps_hold():
            i = k.psfree.pop(0)
            return psb[i], psbuf[i], i

        def ps_release(i):
            k.psfree.append(i)

        def mm(out, lhsT, rhs, start, stop, R, W):
            k.op("pe", lambda e: e.matmul(out, lhsT, rhs, start=start, stop=stop), R=R, W=W)

        xq = [sb("xq%d" % i, [128, D], F32) for i in range(2)]
        b_xq = [Buf("xq%d" % i) for i in range(2)]
        hq = [sb("hq%d" % i, [128, D], BF16) for i in range(2)]
        b_hq = [Buf("hq%d" % i) for i in range(2)]
        hT = [sb("hT%d" % i, [128, 8, 512], BF16) for i in range(2)]
        b_hT = [Buf("hT%d" % i) for i in range(2)]
        st = sb("st", [128, 16], F32)
        b_st = [Buf("st%d" % i) for i in range(16)]
        stn = [0]
        fq = [0]

        def st_next():
            i = stn[0] % 16
            stn[0] += 1
            return st[:, i:i + 1], b_st[i]

        def rstd_of(ss_ap, b_ss, n):
            r_ap, b_r = st_next()
            k.op("act", lambda e: e.activation(out=r_ap, in_=ss_ap, func=AF.Sqrt, scale=1.0 / n, bias=b_eps_ap), R=[b_ss, b_der2], W=[b_r])
            r2_ap, b_r2 = st_next()
            k.op("dve", lambda e: e.reciprocal(out=r2_ap, in_=r_ap), R=[b_r], W=[b_r2])
            return r2_ap, b_r2

        der2 = sb("der2", [128, 2], F32)
        b_der2 = Buf("der2")
        k.op("dve", lambda e: e.memset(der2[:, 0:1], EPS), W=[b_der2])
        b_eps_ap = der2[:, 0:1]

        fst = [sb("fst%d" % i, [128, 12], F32) for i in range(2)]
        b_fst = [[Buf("fst%d_%d" % (i, j)) for j in range(12)] for i in range(2)]
        xq4 = [(xq[0], b_xq[0], "xq0"), (xq[1], b_xq[1], "xq1")]
        fcnt = [0]

        hq4 = []

        def front1(src_d, t0, ntok, par):
            early = len(hq4) >= ntok // 128
            for q in range(ntok // 128):
                xs_, b_xs_, key_ = xq4[q % len(xq4)]
                k.dma("sp", xs_[:], src_d[t0 + q * 128:t0 + (q + 1) * 128, :], W=[b_xs_], key=key_)
                hb, b_hb = hq4[q] if early else (hq[q % 2], b_hq[q % 2])
                k.op("act", lambda e: e.activation(out=hb[:], in_=xs_[:], func=AF.Square, accum_out=fst[par][:, q:q + 1]), R=[b_xs_], W=[b_hb, b_fst[par][q]])
            for q in range(ntok // 128):
                k.op("act", lambda e: e.activation(out=fst[par][:, 4 + q:5 + q], in_=fst[par][:, q:q + 1], func=AF.Sqrt, scale=1.0 / D, bias=b_eps_ap), R=[b_fst[par][q], b_der2], W=[b_fst[par][4 + q]])
                k.op("dve", lambda e: e.reciprocal(out=fst[par][:, 8 + q:9 + q], in_=fst[par][:, 4 + q:5 + q]), R=[b_fst[par][4 + q]], W=[b_fst[par][8 + q]])
            if early and not defer_scale[0]:
                front1b(ntok, par)

        defer_scale = [False]

        def front1b(ntok, par):
            for q in range(ntok // 128):
                xs_, b_xs_, key_ = xq4[q % len(xq4)]
                hb, b_hb = hq4[q]
                k.op("dve", lambda e: e.tensor_scalar(out=hb[:], in0=xs_[:], scalar1=fst[par][:, 8 + q:9 + q], scalar2=None, op0=ALU.mult), R=[b_xs_, b_fst[par][8 + q]], W=[b_hb])

        def front2(ntok, hT_t, b_hT_t, par, alloc=None):
            early = len(hq4) >= ntok // 128
            for q in range(ntok // 128):
                xs_, b_xs_, key_ = xq4[q % len(xq4)]
                hb, b_hb = hq4[q] if early else (hq[q % 2], b_hq[q % 2])
                if not early:
                    k.op("dve", lambda e: e.tensor_scalar(out=hb[:], in0=xs_[:], scalar1=fst[par][:, 8 + q:9 + q], scalar2=None, op0=ALU.mult), R=[b_xs_, b_fst[par][8 + q]], W=[b_hb])
                pt, b_pt = (alloc or ps_next)()
                ptb = pt[:].bitcast(BF16)
                for kk in range(8):
                    k.op("pe", lambda e: e.transpose(ptb[:, kk * 128:(kk + 1) * 128], hb[:, kk * 128:(kk + 1) * 128], ident_b), R=[b_hb, b_cstb], W=[b_pt])
                k.op("act", lambda e: e.activation(out=hT_t[:, :, q * 128:(q + 1) * 128], in_=ptb.rearrange("p (k t) -> p k t", k=8), func=AF.Copy), R=[b_pt], W=[b_hT_t])

        def front(src_d, t0, ntok, hT_t, b_hT_t, par=0):
            front1(src_d, t0, ntok, par)
            front2(ntok, hT_t, b_hT_t, par)

        def load_cast(dst_ap, src_ap, b_dst, ncols, scale_ap=None, R=()):
            for c0 in range(0, ncols, 1024):
                n = min(1024, ncols - c0)
                s = stg_n[0] % 2
                stg_n[0] += 1
                k.dma("sp", stg[s][:, 0:n], src_ap[:, c0:c0 + n], W=[b_stg[s]], key="stg%d" % s)
                if scale_ap is None:
                    k.op("act", lambda e: e.activation(out=dst_ap[:, c0:c0 + n], in_=stg[s][:, 0:n], func=AF.Copy), R=[b_stg[s]] + list(R), W=[b_dst])
                else:
                    k.op("dve", lambda e: e.tensor_scalar(out=dst_ap[:, c0:c0 + n], in0=stg[s][:, 0:n], scalar1=scale_ap, scalar2=None, op0=ALU.mult), R=[b_stg[s], b_pp] + list(R), W=[b_dst])

        stg = [sb("stg%d" % i, [128, 1024], F32) for i in range(2)]
        b_stg = [Buf("stg%d" % i) for i in range(2)]
        stg_n = [0]

        TT = min(512, T)
        NT = T // TT
        NQ = TT // 128

        with ExitStack() as es1:
            def sb1(name, shape, dt):
                return es1.enter_context(nc.sbuf_tensor("sb_" + name, shape, dt))
            w1 = sb1("w1", [128, 8, 2048], BF16)
            b_w1 = Buf("w1")
            wo1 = sb1("wo1", [128, 8, 1024], BF16)
            b_wo1 = Buf("wo1")
            wab = sb1("wab", [128, 2, 8, 128], BF16)
            b_wab = Buf("wab")
            for kk in range(8):
                load_cast(w1[:, kk, :], win_l_d[:, kk, :], b_w1, 2048, scale_ap=pp[:, G1 + kk:G1 + kk + 1])
            for kk in range(8):
                load_cast(wo1[:, kk, :], wout_l_d[:, kk, :], b_wo1, 1024, scale_ap=pp[:, GL + kk:GL + kk + 1])
            load_cast(wab[:, 0, :, :].rearrange("p a b -> p (a b)"), wa_d.rearrange("p a b -> p (a b)"), b_wab, 1024)
            load_cast(wab[:, 1, :, :].rearrange("p a b -> p (a b)"), wx_d.rearrange("p a b -> p (a b)"), b_wab, 1024)

            xqa = [sb1("xqa%d" % i, [128, D], F32) for i in range(2)]
            b_xqa = [Buf("xqa%d" % i) for i in range(2)]
            xq4[:] = [(xq[0], b_xq[0], "xq0"), (xq[1], b_xq[1], "xq1"), (xqa[0], b_xqa[0], "xqa0"), (xqa[1], b_xqa[1], "xqa1")]
            hqa = [sb1("hqa%d" % i, [128, D], BF16) for i in range(2)]
            b_hqa = [Buf("hqa%d" % i) for i in range(2)]
            hq4[:] = [(hq[0], b_hq[0]), (hq[1], b_hq[1]), (hqa[0], b_hqa[0]), (hqa[1], b_hqa[1])]
            NB = 4
            cx = [sb1("cx%d" % i, [128, TT + 3], F32) for i in range(NB)]
            lx = [sb1("lx%d" % i, [128, TT], F32) for i in range(NB)]
            lxb = [sb1("lxb%d" % i, [128, TT], BF16) for i in range(NB)]
            rr = [sb1("rr%d" % i, [128, TT], F32) for i in range(NB)]
            ii = [sb1("ii%d" % i, [128, TT], F32) for i in range(NB)]
            a2 = [sb1("a2%d" % i, [128, TT], F32) for i in range(NB)]
            hs = [sb1("hs%d" % i, [128, TT], F32) for i in range(NB)]
            gg = [sb1("gg%d" % i, [128, TT], F32) for i in range(NB)]
            b_cx = [Buf("cx") for _ in range(NB)]
            b_lx = [Buf("lx") for _ in range(NB)]
            b_lxb = [Buf("lxb") for _ in range(NB)]
            b_rr = [Buf("rr") for _ in range(NB)]
            b_ii = [Buf("ii") for _ in range(NB)]
            b_a2 = [Buf("a2") for _ in range(NB)]
            b_hs = [Buf("hs") for _ in range(NB)]
            b_gg = [Buf("gg") for _ in range(NB)]
            hist = sb1("hist", [128, 8, 3], F32)
            b_hist = [Buf("hist%d" % i) for i in range(8)]
            carry = sb1("carry", [128, 8], F32)
            b_carry = [Buf("carry%d" % i) for i in range(8)]
            k.op("dve", lambda e: e.memset(hist[:], 0.0), W=b_hist)
            k.op("dve", lambda e: e.memset(carry[:], 0.0), W=b_carry)
            yT = sb1("yT1", [128, 8, TT], BF16)
            b_yT = [Buf("yT1_%d" % i) for i in range(8)]
            sq = sb1("sq1", [128, 8, TT], BF16)
            b_sq = [Buf("sq1_%d" % i) for i in range(8)]
            po = [sb1("po%d" % i, [128, D], F32) for i in range(2)]
            b_po = [Buf("po%d" % i) for i in range(2)]
            pon = 0
            blk = 0
            CAST_ENG = "pool"
            for it in range(NT if 1 in sweeps else 0):
                t0 = it * TT
                hs_i = it % 2
                if it == 0:
                    front(x_d, t0, TT, hT[hs_i], b_hT[hs_i], par=0)
                hTt, b_hTt = hT[hs_i], b_hT[hs_i]
                def stA(pr_, hTx, b_hTx, slots):
                    bs = [(2 * pr_ + j, slots[j]) for j in range(2)]
                    for b, s in bs:
                        px, b_px = ps_next()
                        for kk in range(8):
                            mm(px[:, 0:TT], w1[:, kk, b * 128:(b + 1) * 128], hTx[:, kk, 0:TT], kk == 0, kk == 7, [b_w1, b_hTx], [b_px])
                        k.op("dve", lambda e: e.tensor_copy(out=cx[s][:, 0:3], in_=hist[:, b, :]), R=[b_hist[b]], W=[b_cx[s]])
                        k.op("act", lambda e: e.activation(out=cx[s][:, 3:3 + TT], in_=px[:, 0:TT], func=AF.Copy), R=[b_px], W=[b_cx[s]])
                        k.op("dve", lambda e: e.tensor_copy(out=hist[:, b, :], in_=cx[s][:, TT:TT + 3]), R=[b_cx[s]], W=[b_hist[b]])
                        cw = lambda j: pp[:, LCW + b * 4 + j:LCW + b * 4 + j + 1]
                        k.op("dve", lambda e: e.tensor_scalar(out=lx[s][:], in0=cx[s][:, 0:TT], scalar1=cw(0), scalar2=pp[:, LCB + b:LCB + b + 1], op0=ALU.mult, op1=ALU.add), R=[b_cx[s], b_pp], W=[b_lx[s]])
                        for j in range(1, 4):
                            k.op("dve", lambda e: e.scalar_tensor_tensor(out=lx[s][:], in0=cx[s][:, j:j + TT], scalar=cw(j), in1=lx[s][:], op0=ALU.mult, op1=ALU.add), R=[b_cx[s], b_pp, b_lx[s]], W=[b_lx[s]])
                        k.op("dve", lambda e: e.tensor_copy(out=lxb[s][:], in_=lx[s][:]), R=[b_lx[s]], W=[b_lxb[s]])

                def stB(pr_, hTx, b_hTx, slots):
                    bs = [(2 * pr_ + j, slots[j]) for j in range(2)]
                    prs, pis = {}, {}
                    for b, s in bs:
                        pr, b_pr = ps_next()
                        mm(pr[:, 0:TT], wab[:, 0, b, :], lxb[s][:], True, True, [b_wab, b_lxb[s]], [b_pr])
                        pi, b_pi = ps_next()
                        mm(pi[:, 0:TT], wab[:, 1, b, :], lxb[s][:], True, True, [b_wab, b_lxb[s]], [b_pi])
                        prs[b] = (pr, b_pr)
                        pis[b] = (pi, b_pi)
                    for b, s in bs:
                        pr, b_pr = prs[b]
                        pi, b_pi = pis[b]
                        k.op("act", lambda e: e.activation(out=rr[s][:], in_=pr[:, 0:TT], func=AF.Sigmoid, bias=pp[:, BA + b:BA + b + 1]), R=[b_pr, b_pp], W=[b_rr[s]])
                        k.op("act", lambda e: e.activation(out=ii[s][:], in_=pi[:, 0:TT], func=AF.Sigmoid, bias=pp[:, BX + b:BX + b + 1]), R=[b_pi, b_pp], W=[b_ii[s]])
                        k.op("pool", lambda e: e.tensor_tensor(out=ii[s][:], in0=ii[s][:], in1=lx[s][:], op=ALU.mult), R=[b_ii[s], b_lx[s]], W=[b_ii[s]])
                    for b, s in bs:
                        k.op("act", lambda e: e.activation(out=a2[s][:], in_=rr[s][:], func=AF.Exp, scale=der[:, 8 + b:9 + b]), R=[b_rr[s], b_der], W=[b_a2[s]])
                        k.op("act", lambda e: e.activation(out=rr[s][:], in_=rr[s][:], func=AF.Exp, scale=der[:, b:b + 1]), R=[b_rr[s], b_der], W=[b_rr[s]])
                    for b, s in bs:
                        k.op("act", lambda e: e.activation(out=a2[s][:], in_=a2[s][:], func=AF.Sqrt, scale=-1.0, bias=1.0), R=[b_a2[s]], W=[b_a2[s]])
                    for b, s in bs:
                        k.op("dve", lambda e: e.tensor_tensor(out=ii[s][:], in0=ii[s][:], in1=a2[s][:], op=ALU.mult), R=[b_ii[s], b_a2[s]], W=[b_ii[s]])
                        k.op("dve", lambda e: e.tensor_tensor_scan(out=hs[s][:], data0=rr[s][:], data1=ii[s][:], initial=carry[:, b:b + 1], op0=ALU.mult, op1=ALU.add), R=[b_rr[s], b_ii[s], b_carry[b]], W=[b_hs[s]])
                        k.op("dve", lambda e: e.tensor_copy(out=carry[:, b:b + 1], in_=hs[s][:, TT - 1:TT]), R=[b_hs[s]], W=[b_carry[b]])
                    pgs = {}
                    for b, s in bs:
                        pg, b_pg = ps_next()
                        for kk in range(8):
                            mm(pg[:, 0:TT], w1[:, kk, 1024 + b * 128:1024 + (b + 1) * 128], hTx[:, kk, 0:TT], kk == 0, kk == 7, [b_w1, b_hTx], [b_pg])
                        pgs[b] = (pg, b_pg)
                    for b, s in bs:
                        pg, b_pg = pgs[b]
                        k.op("act", lambda e: e.activation(out=gg[s][:], in_=pg[:, 0:TT], func=AF.Gelu_apprx_tanh), R=[b_pg], W=[b_gg[s]])
                    for b, s in bs:
                        k.op("pool", lambda e: e.tensor_tensor(out=gg[s][:], in0=gg[s][:], in1=hs[s][:], op=ALU.mult), R=[b_gg[s], b_hs[s]], W=[b_gg[s]])
                        k.op(CAST_ENG, lambda e: e.tensor_tensor(out=sq[:, b, :], in0=gg[s][:], in1=gg[s][:], op=ALU.mult), R=[b_gg[s]], W=[b_sq[b]])
                        k.op(CAST_ENG, lambda e: e.tensor_copy(out=yT[:, b, :], in_=gg[s][:]), R=[b_gg[s]], W=[b_yT[b]])

                slot_of = lambda pr_: [(2 * (pr_ % 2)) % NB, (2 * (pr_ % 2) + 1) % NB]
                if it == 0:
                    stA(0, hTt, b_hTt, slot_of(0))
                for pr_ in range(4):
                    if pr_ == 0 and it + 1 < NT:
                        defer_scale[0] = True
                        front1(x_d, t0 + TT, TT, (it + 1) % 2)
                        defer_scale[0] = False
                    if pr_ == 1 and it + 1 < NT:
                        front1b(TT, (it + 1) % 2)
                    if pr_ == 2 and it + 1 < NT:
                        front2(TT, hT[1 - hs_i], b_hT[1 - hs_i], (it + 1) % 2)
                    if pr_ + 1 < 4:
                        stA(pr_ + 1, hTt, b_hTt, slot_of(pr_ + 1))
                    elif it + 1 < NT:
                        stA(0, hT[1 - hs_i], b_hT[1 - hs_i], slot_of(0))
                    stB(pr_, hTt, b_hTt, slot_of(pr_))
                for q in range(NQ):
                    pss, b_pss = ps_next()
                    for b in range(8):
                        mm(pss[:, 0:2], sq[:, b, q * 128:(q + 1) * 128], ones_b[:, 0:2], b == 0, b == 7, [b_sq[b], b_cstb], [b_pss])
                    r_ap, b_r = rstd_of(pss[:, 0:1], b_pss, D)
                    o = pon % 2
                    pon += 1
                    for hf in range(2):
                        pm, b_pm = ps_next()
                        for b in range(8):
                            mm(pm[:, :], yT[:, b, q * 128:(q + 1) * 128], wo1[:, b, hf * 512:(hf + 1) * 512], b == 0, b == 7, [b_yT[b], b_wo1], [b_pm])
                        k.op("act", lambda e: e.activation(out=po[o][:, hf * 512:(hf + 1) * 512], in_=pm[:, :], func=AF.Copy, scale=r_ap), R=[b_pm, b_r], W=[b_po[o]])
                    k.dma("sp", pl_d[t0 + q * 128:t0 + (q + 1) * 128, :], po[o][:], R=[b_po[o]], key="po%d" % o, store=True)
            k.barrier()


        junk = sb("junk", [128, 512], BF16)
        b_junk = Buf("junk")

        def rstd_of2(ssa, b_a, ssb, b_b, n):
            t_ap, b_t = st_next()
            k.op("dve", lambda e: e.tensor_tensor(out=t_ap, in0=ssa, in1=ssb, op=ALU.add), R=[b_a, b_b], W=[b_t])
            return rstd_of(t_ap, b_t, n)

        with ExitStack() as es2:
            def sb2(name, shape, dt):
                return es2.enter_context(nc.sbuf_tensor("sb_" + name, shape, dt))
            w2 = sb2("w2", [128, 8, 2576], BF16)
            b_w2 = Buf("w2")
            wo2 = sb2("wo2", [128, 8, 1024], BF16)
            b_wo2 = Buf("wo2")
            for kk in range(8):
                load_cast(w2[:, kk, :], win_s_d[:, kk, :], b_w2, 2576, scale_ap=pp[:, G1 + kk:G1 + kk + 1])
            for kk in range(8):
                load_cast(wo2[:, kk, :], wout_s_d[:, kk, :], b_wo2, 1024, scale_ap=pp[:, GS + kk:GS + kk + 1])
            cx2 = [sb2("cx2%d" % i, [128, TT + 4], BF16) for i in range(3)]
            dgt = sb2("dgt", [128, 48, 128], BF16)
            b_dgt = Buf("dgt")
            for m_ in range(48):
                k.op("dve", lambda e: e.tensor_scalar(out=dgt[:, m_, :], in0=ident_b, scalar1=pp[:, SCW + m_:SCW + m_ + 1], scalar2=None, op0=ALU.mult), R=[b_cstb, b_pp], W=[b_dgt])
            cv = [sb2("cv%d" % i, [128, TT], F32) for i in range(3)]
            b_cx2 = [Buf("cx2") for _ in range(3)]
            b_cv = [Buf("cv") for _ in range(3)]
            hist2 = sb2("hist2", [128, 12, 3], F32)
            b_hist2 = [Buf("hist2_%d" % i) for i in range(12)]
            k.op("dve", lambda e: e.memset(hist2[:], 0.0), W=b_hist2)
            sxT = sb2("sxT", [128, NQ, 1024], F32)
            b_sxT = [Buf("sxT%d" % i) for i in range(NQ)]
            BCb = sb2("BCb", [128, 4, TT], BF16)
            b_BC = [Buf("BC%d" % i) for i in range(4)]
            Btok = sb2("Btok", [128, NQ, 2, 128], BF16)
            b_Btok = [Buf("Btok%d" % i) for i in range(2)]
            zs = [sb2("zs%d" % i, [128, 1024], F32) for i in range(2)]
            b_zs = [Buf("zs%d" % i) for i in range(2)]
            sm = sb2("sm", [128, 8, 64], F32)
            SM_DTR, SM_DT, SM_A, SM_ACS, SM_EACS, SM_DCH, SM_DEC, SM_DTDEC = range(8)
            b_sm = [Buf("sm%d" % i) for i in range(8)]
            scM_ = [sb2("scM%d" % i, [128, 2, 128], BF16) for i in range(2)]
            b_scM_ = [Buf("scM") for _ in range(2)]
            RHSU_ENG = "pool"
            XDEC_ENG = "dve"
            rhsU_ = [sb2("rhsU%d" % i, [128, 8, 128], F32) for i in range(1)] * 2
            b_rhsU_ = [Buf("rhsU")] * 2
            Lb_ = [sb2("Lb%d" % i, [128, 8, 128], BF16) for i in range(2)]
            b_L_ = [Buf("L") for _ in range(2)]
            MT_ = [sb2("MT%d" % i, [128, 8, 128], BF16) for i in range(2)]
            b_MT_ = [Buf("MT") for _ in range(2)]
            xs_ = [sb2("xs%d" % i, [128, 512], BF16) for i in range(2)]
            b_xs_ = [Buf("xs") for _ in range(2)]
            xdec_ = [sb2("xdec%d" % i, [128, 512], BF16) for i in range(2)]
            b_xdec_ = [Buf("xdec") for _ in range(2)]
            xsd_ = [sb2("xsd%d" % i, [128, 512], F32) for i in range(2)]
            b_xsd_ = [Buf("xsd") for _ in range(2)]
            t1 = [sb2("t1%d" % i, [128, 512], F32) for i in range(1)] * 2
            b_t1 = [Buf("t1")] * 2
            ybf = [sb2("ybf%d" % i, [128, 1024], BF16) for i in range(1)] * 2
            b_ybf = [Buf("ybf")] * 2
            Sst = sb2("Sst", [128, 1024], F32)
            b_S = [Buf("S%d" % i) for i in range(2)]
            Sbf = sb2("Sbf", [128, 1024], BF16)
            b_Sbf = [Buf("Sbf%d" % i) for i in range(2)]
            k.op("dve", lambda e: e.memset(Sst[:], 0.0), W=b_S)
            k.op("dve", lambda e: e.memset(Sbf[:], 0.0), W=b_Sbf)
            yT2 = sb2("yT2", [128, 8, TT], BF16)
            b_yT2 = Buf("yT2")
            plq = sb2("plq", [128, 1024], F32)
            b_plq = Buf("plq")
            xr = sb2("xr", [128, 1024], F32)
            b_xr = Buf("xr")
            b_pl_scr = Buf("pl_scr")
            xq4[:] = [(xq[0], b_xq[0], "xq0"), (xq[1], b_xq[1], "xq1"), (stg[0], b_stg[0], "stg0"), (stg[1], b_stg[1], "stg1")]
            hq4[:] = []
            blk = 0
            t1n = 0
            for it in range(NT if 2 in sweeps else 0):
                t0 = it * TT
                hTt, b_hTt = hT[it % 2], b_hT[it % 2]
                if it == 0:
                    front(x_d, t0, TT, hTt, b_hTt, par=0)
                def conv_mm(b):
                    s = b % 3
                    px, b_px = ps_next()
                    c0 = 1024 + b * 128
                    for kk in range(8):
                        mm(px[:, 0:TT], w2[:, kk, c0:c0 + 128], hTt[:, kk, 0:TT], kk == 0, kk == 7, [b_w2, b_hTt], [b_px])
                    k.op("dve", lambda e: e.tensor_copy(out=cx2[s][:, 0:3], in_=hist2[:, b, :]), R=[b_hist2[b]], W=[b_cx2[s]])
                    k.op("act", lambda e: e.activation(out=cx2[s][:, 3:3 + TT], in_=px[:, 0:TT], func=AF.Copy), R=[b_px], W=[b_cx2[s]])
                    k.op("dve", lambda e: e.tensor_copy(out=hist2[:, b, :], in_=cx2[s][:, TT:TT + 3]), R=[b_cx2[s]], W=[b_hist2[b]])
                    pc, b_pc = ps_next()
                    for j in range(4):
                        mm(pc[:, 0:TT], dgt[:, b * 4 + j, :], cx2[s][:, j:j + TT], j == 0, j == 3, [b_dgt, b_cx2[s]], [b_pc])
                    if b < 8:
                        k.op("act", lambda e: e.activation(out=cv[s][:], in_=pc[:, 0:TT], func=AF.Silu, bias=pp[:, SCB + b:SCB + b + 1]), R=[b_pc, b_pp], W=[b_cv[s]])
                    else:
                        j4 = b - 8
                        k.op("act", lambda e: e.activation(out=BCb[:, j4, :], in_=pc[:, 0:TT], func=AF.Silu, bias=pp[:, SCB + b:SCB + b + 1]), R=[b_pc, b_pp], W=[b_BC[j4]])

                def conv_tr(b):
                    s = b % 3
                    if b < 8:
                        pt, b_pt = ps_next()
                        for q in range(NQ):
                            k.op("pe", lambda e: e.transpose(pt[:, q * 128:(q + 1) * 128], cv[s][:, q * 128:(q + 1) * 128], ident_f), R=[b_cv[s], b_cst], W=[b_pt])
                        k.op("act", lambda e: e.activation(out=sxT[:, :, b * 128:(b + 1) * 128], in_=pt[:, 0:NQ * 128].rearrange("p (q c) -> p q c", q=NQ), func=AF.Copy), R=[b_pt], W=b_sxT)
                    elif b < 10:
                        j4 = b - 8
                        pt, b_pt = ps_next()
                        ptb = pt[:].bitcast(BF16)
                        for q in range(NQ):
                            k.op("pe", lambda e: e.transpose(ptb[:, q * 128:(q + 1) * 128], BCb[:, j4, q * 128:(q + 1) * 128], ident_b), R=[b_BC[j4], b_cstb], W=[b_pt])
                        k.op("act", lambda e: e.activation(out=Btok[:, :, j4, :], in_=ptb[:, 0:NQ * 128].rearrange("p (q c) -> p q c", q=NQ), func=AF.Copy), R=[b_pt], W=[b_Btok[j4]])

                for b in range(12 + 2):
                    if b < 12:
                        conv_mm(b)
                    if b >= 2:
                        conv_tr(b - 2)
                pd, b_pd = ps_next()
                for q in range(NQ):
                    for kk in range(8):
                        mm(pd[:, q * 16:(q + 1) * 16], hTt[:, kk, q * 128:(q + 1) * 128], w2[:, kk, 2560:2576], kk == 0, kk == 7, [b_w2, b_hTt], [b_pd])
                NH = NQ * 16
                bq = lambda ap16: ap16.unsqueeze(1).to_broadcast([128, NQ, 16])
                v3 = lambda i: sm[:, i, 0:NH].rearrange("p (q h) -> p q h", q=NQ)
                k.op("dve", lambda e: e.tensor_tensor(out=v3(SM_DTR), in0=pd[:, 0:NH].rearrange("p (q h) -> p q h", q=NQ), in1=bq(bc[:, DTB:DTB + 16]), op=ALU.add), R=[b_pd, b_bc], W=[b_sm[SM_DTR]])
                k.op("act", lambda e: e.activation(out=sm[:, SM_DTR, 0:NH], in_=sm[:, SM_DTR, 0:NH], func=AF.Exp), R=[b_sm[SM_DTR]], W=[b_sm[SM_DTR]])
                k.op("act", lambda e: e.activation(out=sm[:, SM_DT, 0:NH], in_=sm[:, SM_DTR, 0:NH], func=AF.Ln, bias=1.0), R=[b_sm[SM_DTR]], W=[b_sm[SM_DT]])
                k.op("dve", lambda e: e.tensor_tensor(out=v3(SM_A), in0=v3(SM_DT), in1=bq(der[:, 16:32]), op=ALU.mult), R=[b_sm[SM_DT], b_der], W=[b_sm[SM_A]])
                pc, b_pc = ps_next()
                mm(pc[:, 0:NH], U_f, sm[:, SM_A, 0:NH], True, True, [b_cst, b_sm[SM_A]], [b_pc])
                mm(pc[:, 64:64 + NH], ones_f, sm[:, SM_A, 0:NH], True, True, [b_cst, b_sm[SM_A]], [b_pc])
                k.op("act", lambda e: e.activation(out=sm[:, SM_ACS, 0:NH], in_=pc[:, 0:NH], func=AF.Copy), R=[b_pc], W=[b_sm[SM_ACS]])
                k.op("act", lambda e: e.activation(out=sm[:, SM_EACS, 0:NH], in_=pc[:, 0:NH], func=AF.Exp), R=[b_pc], W=[b_sm[SM_EACS]])
                k.op("act", lambda e: e.activation(out=sm[:, SM_DCH, 0:NH], in_=pc[:, 64:64 + NH], func=AF.Exp), R=[b_pc], W=[b_sm[SM_DCH]])
                k.op("dve", lambda e: e.tensor_tensor(out=sm[:, SM_DEC, 0:NH], in0=pc[:, 64:64 + NH], in1=sm[:, SM_ACS, 0:NH], op=ALU.subtract), R=[b_pc, b_sm[SM_ACS]], W=[b_sm[SM_DEC]])
                k.op("act", lambda e: e.activation(out=sm[:, SM_DEC, 0:NH], in_=sm[:, SM_DEC, 0:NH], func=AF.Exp), R=[b_sm[SM_DEC]], W=[b_sm[SM_DEC]])
                k.op("dve", lambda e: e.tensor_tensor(out=sm[:, SM_DTDEC, 0:NH], in0=sm[:, SM_DEC, 0:NH], in1=sm[:, SM_DT, 0:NH], op=ALU.mult), R=[b_sm[SM_DEC], b_sm[SM_DT]], W=[b_sm[SM_DTDEC]])
                items = [(q, g) for q in range(NQ) for g in range(2)]
                NI = len(items)
                held = {}
                ssq_all = {}

                def St0(n):
                    q, g = items[n]
                    qs = slice(q * 128, (q + 1) * 128)
                    if g == 0:
                        zsl = q % 2
                        for hf in range(2):
                            pz, b_pz = ps_next()
                            for kk in range(8):
                                mm(pz[:, :], hTt[:, kk, qs], w2[:, kk, hf * 512:(hf + 1) * 512], kk == 0, kk == 7, [b_hTt, b_w2], [b_pz])
                            k.op("act", lambda e: e.activation(out=zs[zsl][:, hf * 512:(hf + 1) * 512], in_=pz[:, :], func=AF.Silu), R=[b_pz], W=[b_zs[zsl]])
                        psc, b_psc = ps_next()
                        for g2 in range(2):
                            mm(psc[:, g2 * 128:(g2 + 1) * 128], BCb[:, g2, qs], BCb[:, 2 + g2, qs], True, True, [b_BC[g2], b_BC[2 + g2]], [b_psc])
                        k.op("dve", lambda e: e.tensor_tensor(out=scM_[q % 2][:], in0=psc[:, 0:256].rearrange("p (g l) -> p g l", g=2), in1=U_f.unsqueeze(1).to_broadcast([128, 2, 128]), op=ALU.mult), R=[b_psc, b_cst], W=[b_scM_[q % 2]])
                    hc = q * 16 + g * 8
                    sl = n % 2
                    k.op(RHSU_ENG, lambda e: e.tensor_tensor(out=rhsU_[sl][:], in0=U_f.unsqueeze(1).to_broadcast([128, 8, 128]), in1=sm[:, SM_A, hc:hc + 8].unsqueeze(2).to_broadcast([128, 8, 128]), op=ALU.mult), R=[b_cst, b_sm[SM_A]], W=[b_rhsU_[sl]])
                    for hh in range(2):
                        pL, b_pL = ps_next()
                        mm(pL[:, :], SL_f, rhsU_[sl][:, hh * 4:(hh + 1) * 4, :].rearrange("p h l -> p (h l)"), True, True, [b_cst, b_rhsU_[sl]], [b_pL])
                        k.op("act", lambda e: e.activation(out=Lb_[sl][:, hh * 4:(hh + 1) * 4, :].rearrange("p h l -> p (h l)"), in_=pL[:, :], func=AF.Exp), R=[b_pL], W=[b_L_[sl]])

                def St1(n):
                    q, g = items[n]
                    qs = slice(q * 128, (q + 1) * 128)
                    gsl = slice(g * 512, (g + 1) * 512)
                    hc = q * 16 + g * 8
                    sl = n % 2
                    colb = lambda i, nn: sm[:, i, hc:hc + 8].unsqueeze(2).to_broadcast([128, 8, nn])
                    MT, b_MT, xs, b_xs, xdec, b_xdec = MT_[sl], b_MT_[sl], xs_[sl], b_xs_[sl], xdec_[sl], b_xdec_[sl]
                    k.op("dve", lambda e: e.tensor_tensor(out=MT[:], in0=Lb_[sl][:], in1=scM_[q % 2][:, g, :].unsqueeze(1).to_broadcast([128, 8, 128]), op=ALU.mult), R=[b_L_[sl], b_scM_[q % 2]], W=[b_MT])
                    sx3 = sxT[:, q, gsl].rearrange("p (h c) -> p h c", h=8)
                    k.op(XDEC_ENG, lambda e: e.tensor_tensor(out=xs[:].rearrange("p (h c) -> p h c", h=8), in0=sx3, in1=colb(SM_DT, 64), op=ALU.mult), R=[b_sxT[q], b_sm[SM_DT]], W=[b_xs])
                    k.op(XDEC_ENG, lambda e: e.tensor_tensor(out=xdec[:].rearrange("p (h c) -> p h c", h=8), in0=sx3, in1=colb(SM_DTDEC, 64), op=ALU.mult), R=[b_sxT[q], b_sm[SM_DTDEC]], W=[b_xdec])
                    dsk = bc[:, DSK + g * 8:DSK + g * 8 + 8].unsqueeze(2).to_broadcast([128, 8, 64])
                    k.op(XDEC_ENG, lambda e: e.tensor_tensor(out=xsd_[sl][:].rearrange("p (h c) -> p h c", h=8), in0=sx3, in1=dsk, op=ALU.mult), R=[b_sxT[q], b_bc], W=[b_xsd_[sl]])
                    pyd, b_pyd, i1 = ps_hold()
                    for h in range(8):
                        mm(pyd[:, h * 64:(h + 1) * 64], MT[:, h, :], xs[:, h * 64:(h + 1) * 64], True, True, [b_MT, b_xs], [b_pyd])
                    pyo, b_pyo, i2 = ps_hold()
                    mm(pyo[:, :], BCb[:, 2 + g, qs], Sbf[:, gsl], True, True, [b_BC[2 + g], b_Sbf[g]], [b_pyo])
                    pst, b_pst, i3 = ps_hold()
                    mm(pst[:, :], Btok[:, q, g, :], xdec[:], True, True, [b_Btok[g], b_xdec], [b_pst])
                    held[n] = (pyd, b_pyd, i1, pyo, b_pyo, i2, pst, b_pst, i3)

                def St2(n):
                    q, g = items[n]
                    qs = slice(q * 128, (q + 1) * 128)
                    gsl = slice(g * 512, (g + 1) * 512)
                    hc = q * 16 + g * 8
                    sl = n % 2
                    zsl = q % 2
                    yb = q % 2
                    colb = lambda i, nn: sm[:, i, hc:hc + 8].unsqueeze(2).to_broadcast([128, 8, nn])
                    pyd, b_pyd, i1, pyo, b_pyo, i2, pst, b_pst, i3 = held.pop(n)
                    ts_ = n % 2
                    tt3 = t1[ts_][:].rearrange("p (h c) -> p h c", h=8)
                    k.op("dve", lambda e: e.tensor_tensor(out=tt3, in0=pyo[:, :].rearrange("p (h c) -> p h c", h=8), in1=colb(SM_EACS, 64), op=ALU.mult), R=[b_pyo, b_sm[SM_EACS]], W=[b_t1[ts_]])
                    k.op("dve", lambda e: e.tensor_tensor(out=t1[ts_][:], in0=t1[ts_][:], in1=pyd[:, :], op=ALU.add), R=[b_t1[ts_], b_pyd], W=[b_t1[ts_]])
                    k.op("dve", lambda e: e.tensor_tensor(out=t1[ts_][:], in0=t1[ts_][:], in1=xsd_[sl][:], op=ALU.add), R=[b_t1[ts_], b_xsd_[sl]], W=[b_t1[ts_]])
                    k.op("dve", lambda e: e.tensor_tensor(out=t1[ts_][:], in0=t1[ts_][:], in1=zs[zsl][:, gsl], op=ALU.mult), R=[b_t1[ts_], b_zs[zsl]], W=[b_t1[ts_]])
                    ss_ap, b_ss = st_next()
                    k.op("act", lambda e: e.activation(out=junk[:], in_=t1[ts_][:], func=AF.Square, accum_out=ss_ap), R=[b_t1[ts_]], W=[b_junk, b_ss])
                    ssq_all[(q, g)] = (ss_ap, b_ss)
                    k.op("act", lambda e: e.activation(out=ybf[yb][:, gsl], in_=t1[ts_][:], func=AF.Copy), R=[b_t1[ts_]], W=[b_ybf[yb]])
                    S3 = Sst[:, gsl].rearrange("p (h c) -> p h c", h=8)
                    k.op(XDEC_ENG, lambda e: e.tensor_tensor(out=S3, in0=S3, in1=colb(SM_DCH, 64), op=ALU.mult), R=[b_S[g], b_sm[SM_DCH]], W=[b_S[g]])
                    k.op("dve", lambda e: e.tensor_tensor(out=Sst[:, gsl], in0=Sst[:, gsl], in1=pst[:, :], op=ALU.add), R=[b_S[g], b_pst], W=[b_S[g]])
                    k.op("act", lambda e: e.activation(out=Sbf[:, gsl], in_=Sst[:, gsl], func=AF.Copy), R=[b_S[g]], W=[b_Sbf[g]])
                    ps_release(i1)
                    ps_release(i2)
                    ps_release(i3)
                    if g == 1:
                        St3(q)

                def St3(q):
                    qs = slice(q * 128, (q + 1) * 128)
                    yb = q % 2
                    a0, a1 = ssq_all.pop((q, 0)), ssq_all.pop((q, 1))
                    r_s, b_rs = rstd_of2(a0[0], a0[1], a1[0], a1[1], D)
                    pt, b_pt = ps_next()
                    ptb = pt[:].bitcast(BF16)
                    for kk in range(8):
                        k.op("pe", lambda e: e.transpose(ptb[:, kk * 128:(kk + 1) * 128], ybf[yb][:, kk * 128:(kk + 1) * 128], ident_b), R=[b_ybf[yb], b_cstb], W=[b_pt])
                    k.op("act", lambda e: e.activation(out=yT2[:, :, qs], in_=ptb.rearrange("p (k t) -> p k t", k=8), func=AF.Copy), R=[b_pt], W=[b_yT2])
                    k.dma("sp", plq[:], pl_d[t0 + q * 128:t0 + (q + 1) * 128, :], W=[b_plq], key="plq")
                    k.dma("sp", xr[:], x_d[t0 + q * 128:t0 + (q + 1) * 128, :], W=[b_xr], key="xr")
                    ssm = []
                    for hf in range(2):
                        hsl = slice(hf * 512, (hf + 1) * 512)
                        pm, b_pm = ps_next()
                        for b in range(8):
                            mm(pm[:, :], yT2[:, b, qs], wo2[:, b, hsl], b == 0, b == 7, [b_yT2, b_wo2], [b_pm])
                        k.op("dve", lambda e: e.scalar_tensor_tensor(out=plq[:, hsl], in0=pm[:, :], scalar=r_s, in1=plq[:, hsl], op0=ALU.mult, op1=ALU.add), R=[b_pm, b_rs, b_plq], W=[b_plq])
                        ss_ap, b_ss = st_next()
                        k.op("act", lambda e: e.activation(out=junk[:], in_=plq[:, hsl], func=AF.Square, accum_out=ss_ap), R=[b_plq], W=[b_junk, b_ss])
                        ssm.append((ss_ap, b_ss))
                    r_p, b_rp = rstd_of2(ssm[0][0], ssm[0][1], ssm[1][0], ssm[1][1], D)
                    k.op("dve", lambda e: e.scalar_tensor_tensor(out=plq[:], in0=plq[:], scalar=r_p, in1=bc[:, GP1:GP1 + 1024], op0=ALU.mult, op1=ALU.mult), R=[b_plq, b_rp, b_bc], W=[b_plq])
                    k.op("dve", lambda e: e.tensor_tensor(out=xr[:], in0=xr[:], in1=plq[:], op=ALU.add), R=[b_xr, b_plq], W=[b_xr])
                    k.dma("sp", x1_d[t0 + q * 128:t0 + (q + 1) * 128, :], xr[:], R=[b_xr], key="xr_st", store=True)

                for step in range(NI + 2):
                    if step == 0 and it + 1 < NT:
                        front1(x_d, t0 + TT, TT, (it + 1) % 2)
                    if step == 6 and it + 1 < NT:
                        front2(TT, hT[(it + 1) % 2], b_hT[(it + 1) % 2], (it + 1) % 2)
                    if 2 <= step:
                        St2(step - 2)
                    if step < NI:
                        St0(step)
                    if 1 <= step < NI + 1:
                        St1(step - 1)
            k.barrier()

        with ExitStack() as es3:
            def sb3(name, shape, dt):
                return es3.enter_context(nc.sbuf_tensor("sb_" + name, shape, dt))
            wdn = sb3("wdn", [128, NF, 1024], BF16)
            b_wdn = Buf("wdn")
            for f in range(NF):
                load_cast(wdn[:, f, :], wdn_d[:, f, :], b_wdn, 1024)
            wgc = [sb3("wgc%d" % i, [128, 8, 256], BF16) for i in range(2)]
            b_wgc = [Buf("wgc%d" % i) for i in range(2)]
            b_wgs = [Buf("wgu_scr%d" % i) for i in range(NF)]
            for f in range(NF):
                c2 = f % 2
                for kh in range(2):
                    s = stg_n[0] % 2
                    stg_n[0] += 1
                    k.dma("sp", stg[s][:, 0:1024].rearrange("p (k c) -> p k c", k=4), wgu_d[f, :, kh * 4:(kh + 1) * 4, :], W=[b_stg[s]], key="stg%d" % s)
                    for k4 in range(4):
                        kk = kh * 4 + k4
                        k.op("dve" if k4 % 2 == 0 else "act",
                             (lambda e: e.tensor_scalar(out=wgc[c2][:, kk, :], in0=stg[s][:, k4 * 256:(k4 + 1) * 256], scalar1=pp[:, G2 + kk:G2 + kk + 1], scalar2=None, op0=ALU.mult)) if k4 % 2 == 0 else
                             (lambda e: e.activation(out=wgc[c2][:, kk, :], in_=stg[s][:, k4 * 256:(k4 + 1) * 256], func=AF.Copy, scale=pp[:, G2 + kk:G2 + kk + 1])),
                             R=[b_stg[s], b_pp], W=[b_wgc[c2]])
                k.dma("sp", wgu_s[f], wgc[c2][:].rearrange("p k c -> p (k c)"), R=[b_wgc[c2]], W=[b_wgs[f]], key="wgs%d" % c2)
            k.barrier()
            xqc = [sb3("xqc%d" % i, [128, D], F32) for i in range(2)]
            b_xqc = [Buf("xqc%d" % i) for i in range(2)]
            xq4[:] = [(xq[0], b_xq[0], "xq0"), (xq[1], b_xq[1], "xq1"), (xqc[0], b_xqc[0], "xqc0"), (xqc[1], b_xqc[1], "xqc1")]
            hqc = [sb3("hqc%d" % i, [128, D], BF16) for i in range(2)]
            b_hqc = [Buf("hqc%d" % i) for i in range(2)]
            hq4[:] = [(hq[0], b_hq[0]), (hq[1], b_hq[1]), (hqc[0], b_hqc[0]), (hqc[1], b_hqc[1])]
            NR = 6
            wr = [sb3("wr%d" % i, [128, 8, 256], BF16) for i in range(NR)]
            b_wr = [Buf("wr%d" % i) for i in range(NR)]
            gT = sb3("gT", [128, NF, TT], BF16)
            b_gT = Buf("gT")
            sg = [sb3("sg%d" % i, [128, TT], F32) for i in range(2)]
            b_sg = [Buf("sg%d" % i) for i in range(2)]
            ob = [sb3("ob%d" % i, [128, 1024], F32) for i in range(2)]
            b_ob = [Buf("ob%d" % i) for i in range(2)]
            xr3 = [sb3("xr3%d" % i, [128, 1024], F32) for i in range(2)]
            b_xr3 = [Buf("xr3%d" % i) for i in range(2)]
            wn = 0
            on = 0
            for it in range(NT if 3 in sweeps else 0):
                t0 = it * TT
                hTt, b_hTt = hT[it % 2], b_hT[it % 2]
                if it == 0:
                    front(x1_d, t0, TT, hTt, b_hTt, par=0)
                for f in range(NF):
                    if f == 0 and it + 1 < NT:
                        defer_scale[0] = True
                        front1(x1_d, t0 + TT, TT, (it + 1) % 2)
                        defer_scale[0] = False
                    if f == 6 and it + 1 < NT:
                        front1b(TT, (it + 1) % 2)
                    if f == 14 and it + 1 < NT:
                        front2(TT, hT[(it + 1) % 2], b_hT[(it + 1) % 2], (it + 1) % 2)
                    sl = wn % NR
                    wn += 1
                    k.dma("sp", wr[sl][:].rearrange("p k c -> p (k c)"), wgu_s[f], R=[b_wgs[f]], W=[b_wr[sl]], key="wr%d" % sl)
                    pg, b_pg = ps_next()
                    for kk in range(8):
                        mm(pg[:, 0:TT], wr[sl][:, kk, 0:128], hTt[:, kk, 0:TT], kk == 0, kk == 7, [b_wr[sl], b_hTt], [b_pg])
                    pu, b_pu = ps_next()
                    for kk in range(8):
                        mm(pu[:, 0:TT], wr[sl][:, kk, 128:256], hTt[:, kk, 0:TT], kk == 0, kk == 7, [b_wr[sl], b_hTt], [b_pu])
                    s2 = f % 2
                    k.op("act", lambda e: e.activation(out=sg[s2][:], in_=pg[:, 0:TT], func=AF.Silu), R=[b_pg], W=[b_sg[s2]])
                    k.op("dve", lambda e: e.tensor_tensor(out=gT[:, f, :], in0=sg[s2][:], in1=pu[:, 0:TT], op=ALU.mult), R=[b_sg[s2], b_pu], W=[b_gT])
                for q in range(NQ):
                    qs = slice(q * 128, (q + 1) * 128)
                    o = on % 2
                    on += 1
                    k.dma("sp", xr3[o][:], x1_d[t0 + q * 128:t0 + (q + 1) * 128, :], W=[b_xr3[o]], key="xr3%d" % o)
                    pms = []
                    for hf in range(2):
                        hsl = slice(hf * 512, (hf + 1) * 512)
                        pm, b_pm = ps_next()
                        for f in range(NF):
                            mm(pm[:, :], gT[:, f, qs], wdn[:, f, hsl], f == 0, f == NF - 1, [b_gT, b_wdn], [b_pm])
                        ss_ap, b_ss = st_next()
                        k.op("act", lambda e: e.activation(out=junk[:], in_=pm[:, :], func=AF.Square, accum_out=ss_ap), R=[b_pm], W=[b_junk, b_ss])
                        pms.append((pm, b_pm, ss_ap, b_ss))
                    r_p, b_rp = rstd_of2(pms[0][2], pms[0][3], pms[1][2], pms[1][3], D)
                    for hf in range(2):
                        hsl = slice(hf * 512, (hf + 1) * 512)
                        pm, b_pm = pms[hf][0], pms[hf][1]
                        k.op("dve", lambda e: e.scalar_tensor_tensor(out=ob[o][:, hsl], in0=pm[:, :], scalar=r_p, in1=bc[:, GP2 + hf * 512:GP2 + (hf + 1) * 512], op0=ALU.mult, op1=ALU.mult), R=[b_pm, b_rp, b_bc], W=[b_ob[o]])
                    k.op("dve", lambda e: e.tensor_tensor(out=ob[o][:], in0=ob[o][:], in1=xr3[o][:], op=ALU.add), R=[b_ob[o], b_xr3[o]], W=[b_ob[o]])
                    k.dma("sp", out_d[t0 + q * 128:t0 + (q + 1) * 128, :], ob[o][:], R=[b_ob[o]], key="ob%d" % o, store=True)
            k.barrier()

        k._wait("sp", k.stores)
    return nc


def _prep_shared(inp):
    f = np.float32
    g = lambda n: np.ascontiguousarray(np.asarray(inp[n], dtype=f)[0])
    w_in = g("w_in")
    pk = lambda w: np.ascontiguousarray(w.reshape(w.shape[0] // 128, 128, w.shape[1]).transpose(1, 0, 2))
    sh = {}
    sh["w_in_l"] = pk(w_in[:, 0:2048])
    sh["w_in_s"] = pk(w_in[:, 2048:4624])
    w_out = g("w_out")
    sh["w_out_l"] = pk(w_out[0:1024])
    sh["w_out_s"] = pk(w_out[1024:2048])
    wg, wu = pk(g("w_gate")), pk(g("w_up"))
    wgu = np.zeros((NF, 128, 8, 256), f)
    for j in range(NF):
        wgu[j, :, :, 0:128] = wg[:, :, j * 128:(j + 1) * 128]
        wgu[j, :, :, 128:256] = wu[:, :, j * 128:(j + 1) * 128]
    sh["w_gu"] = wgu
    sh["w_down"] = pk(g("w_down"))

    def blockdiag(w):
        m = np.zeros((128, 8, 128), f)
        for b in range(8):
            m[0:64, b, 0:64] = w[2 * b]
            m[64:128, b, 64:128] = w[2 * b + 1]
        return m
    sh["wa"] = blockdiag(g("lru_wa"))
    sh["wx"] = blockdiag(g("lru_wx"))
    cm = lambda v: np.ascontiguousarray(v.reshape(-1, 128).T)
    pp = np.zeros((128, 156), f)
    pp[:, 0:8] = cm(g("pre_mix_norm"))
    pp[:, 8:16] = cm(g("pre_ffn_norm"))
    pp[:, 16:24] = cm(g("lru_out_norm"))
    pp[:, 24:32] = cm(g("ssd_out_norm"))
    lcw = g("lru_conv_w")
    for b in range(8):
        for j in range(4):
            pp[:, 32 + b * 4 + j] = lcw[j, b * 128:(b + 1) * 128]
    pp[:, 64:72] = cm(g("lru_conv_b"))
    pp[:, 72:80] = cm(g("lru_ba"))
    pp[:, 80:88] = cm(g("lru_bx"))
    pp[:, 88:96] = cm(g("lru_lambda"))
    scw = g("ssd_conv_w")
    for b in range(12):
        for j in range(4):
            pp[:, 96 + b * 4 + j] = scw[j, b * 128:(b + 1) * 128]
    pp[:, 144:156] = cm(g("ssd_conv_b"))
    sh["pp"] = pp
    bcv = np.concatenate([g("post_mix_norm"), g("post_ffn_norm"), g("ssd_dt_bias"), g("ssd_a_log"), g("ssd_d")])
    sh["bc"] = np.ascontiguousarray(np.broadcast_to(bcv[None, :], (128, bcv.shape[0]))).astype(f)
    cst = np.zeros((128, 4, 128), f)
    i = np.arange(128)
    cst[:, 0, :] = np.eye(128, dtype=f)
    cst[:, 1, :] = (i[:, None] <= i[None, :])
    cst[:, 2, :] = (i[:, None] > i[None, :])
    cst[:, 3, :] = 1.0
    sh["cst"] = cst
    return sh


def kernel(**inputs):
    x = np.asarray(inputs["x"], dtype=np.float32)
    B, T, _ = x.shape
    sh = _prep_shared(inputs)
    nc = build(T)
    in_maps = []
    for c in range(B):
        m = dict(sh)
        m["x"] = np.ascontiguousarray(x[c])
        in_maps.append(m)
    res = run_bass_kernel_spmd(nc, in_maps, core_ids=list(range(B)))
    return np.stack([np.asarray(r["out"]) for r in res.results], axis=0).astype(np.float32)
```
